# Optimizing a Trainium2 kernel written in Bass

```python
import functools
import jax
import jax.numpy as jnp
from jax import lax
import numpy as np

D_MODEL = 2048
BATCH = 8
SEQ = 4096
DEPTH = 4
DEC_BATCH = 8
DEC_SEQ = 16
PAST_LEN = 4096

CHUNK = 64
Q_BLOCK = 128
FOX_HEADS = 8
FOX_HEAD_DIM = 128
FOX_WIDTH = FOX_HEADS * FOX_HEAD_DIM
GDN_HEADS = 8
GDN_DK = 128
GDN_DV = 128
GDN_KEY_WIDTH = GDN_HEADS * GDN_DK
GDN_VAL_WIDTH = GDN_HEADS * GDN_DV
GDN_CONV_CH = 2 * GDN_KEY_WIDTH + GDN_VAL_WIDTH
CONV_W = 4
PLE_DIM = 256
LN_EPS = 1e-5
RMS_EPS = 1e-6
FOX_SCALE = FOX_HEAD_DIM ** -0.5
GDN_SCALE = GDN_DK ** -0.5
DEEPNORM_ALPHA = (2 * DEPTH) ** 0.25
DEEPNORM_BETA = (8 * DEPTH) ** -0.25
SEGMENTS = (FOX_WIDTH, FOX_WIDTH, FOX_WIDTH, FOX_HEADS, FOX_WIDTH,
            GDN_CONV_CH, GDN_HEADS, GDN_HEADS, GDN_VAL_WIDTH,
            D_MODEL, D_MODEL)
SPLIT_IDX = tuple(int(i) for i in np.cumsum(SEGMENTS)[:-1])
N_IN = int(sum(SEGMENTS))

kernel_name = 'fox_gdn_parallel_streaming_encoder'


def layer_norm(x, g, b):
    xf = x.astype(jnp.float32)
    xc = xf - jnp.mean(xf, axis=-1, keepdims=True)
    var = jnp.mean(xc * xc, axis=-1, keepdims=True)
    return (xc * lax.rsqrt(var + LN_EPS) * g + b).astype(x.dtype)


def rms_norm(x, w):
    xf = x.astype(jnp.float32)
    return (xf * lax.rsqrt(jnp.mean(xf * xf, axis=-1, keepdims=True) + RMS_EPS) * w).astype(x.dtype)


def l2_normalize(x):
    xf = x.astype(jnp.float32)
    return xf * lax.rsqrt(jnp.sum(xf * xf, axis=-1, keepdims=True) + RMS_EPS)


def fox_attend(q, k, v, c_q, c_k, q_pos, k_pos):
    s = jnp.einsum('bqhd,bkhd->bhqk', q, k, preferred_element_type=jnp.float32) * FOX_SCALE
    bias = jnp.swapaxes(c_q, 1, 2)[:, :, :, None] - jnp.swapaxes(c_k, 1, 2)[:, :, None, :]
    visible = k_pos[None, :] <= q_pos[:, None]
    p = jax.nn.softmax(jnp.where(visible, s + bias, -jnp.inf), axis=-1)
    return jnp.einsum('bhqk,bkhd->bqhd', p.astype(v.dtype), v)


def gated_delta_chunked(q, k, v, g, beta, s0):
    bsz, n_tok, h = q.shape[:3]
    dv = v.shape[-1]
    c = min(CHUNK, n_tok)
    n = n_tok // c

    def to_chunks(t):
        t = t.reshape((bsz, n, c, h) + t.shape[3:])
        return jnp.moveaxis(t, (1, 3), (0, 2))

    qc, kc, vc, bc = to_chunks(q), to_chunks(k), to_chunks(v), to_chunks(beta)
    gc = jnp.cumsum(to_chunks(g), axis=-1)
    idx = jnp.arange(c)
    incl = idx[:, None] >= idx[None, :]
    strict = idx[:, None] > idx[None, :]
    decay = jnp.exp(jnp.where(incl, gc[..., :, None] - gc[..., None, :], -jnp.inf))
    kb = kc * bc[..., None]
    lmat = jnp.where(strict, jnp.einsum('nbhid,nbhjd->nbhij', kb, kc) * decay, 0.0)
    amat = lmat + jnp.eye(c, dtype=lmat.dtype)
    solve = functools.partial(lax.linalg.triangular_solve, left_side=True, lower=True)
    u = solve(amat, vc * bc[..., None])
    w = solve(amat, kb * jnp.exp(gc)[..., None])
    qk = jnp.where(incl, jnp.einsum('nbhid,nbhjd->nbhij', qc, kc) * decay, 0.0)
    q_dec = qc * jnp.exp(gc)[..., None]
    k_tail = kc * jnp.exp(gc[..., -1:] - gc)[..., None]
    g_last = jnp.exp(gc[..., -1])

    def step(s, xs):
        q_n, w_n, u_n, qk_n, kt_n, gl_n = xs
        v_new = u_n - jnp.einsum('bhcd,bhde->bhce', w_n, s)
        o = jnp.einsum('bhcd,bhde->bhce', q_n, s) + jnp.einsum('bhij,bhje->bhie', qk_n, v_new)
        s = s * gl_n[..., None, None] + jnp.einsum('bhcd,bhce->bhde', kt_n, v_new)
        return s, o

    s_final, o = lax.scan(step, s0, (q_dec, w, u, qk, k_tail, g_last))
    o = jnp.moveaxis(o, (0, 2), (1, 3)).reshape(bsz, n_tok, h, dv)
    return o, s_final


def layer(x, p_l, w_in, fox_f_bias, gdn_conv_w, gdn_a_log, gdn_dt_bias, gdn_norm_w,
          w_out_fox, w_out_gdn, w_out, ln_g, ln_b, w_pl_proj, w_pl_gate, pl_norm_w,
          fox_past, conv_buf, s0):
    bsz, n_tok = x.shape[:2]
    z = x @ w_in
    q_a, k_a, v_a, f_a, gate_a, qkv_b, a_b, beta_b, gate_b, m_a, m_b = jnp.split(z, SPLIT_IDX, axis=-1)

    q_a = q_a.reshape(bsz, n_tok, FOX_HEADS, FOX_HEAD_DIM)
    k_a = k_a.reshape(bsz, n_tok, FOX_HEADS, FOX_HEAD_DIM)
    v_a = v_a.reshape(bsz, n_tok, FOX_HEADS, FOX_HEAD_DIM)
    logf = jax.nn.log_sigmoid(f_a.astype(jnp.float32) + fox_f_bias)
    if fox_past is None:
        c = jnp.cumsum(logf, axis=1)
        pos = jnp.arange(n_tok)
        nb = n_tok // Q_BLOCK
        qb = jnp.moveaxis(q_a.reshape(bsz, nb, Q_BLOCK, FOX_HEADS, FOX_HEAD_DIM), 1, 0)
        cb = jnp.moveaxis(c.reshape(bsz, nb, Q_BLOCK, FOX_HEADS), 1, 0)
        pb = pos.reshape(nb, Q_BLOCK)
        o_a = lax.map(lambda t: fox_attend(t[0], k_a, v_a, t[1], c, t[2], pos), (qb, cb, pb))
        o_a = jnp.moveaxis(o_a, 0, 1).reshape(bsz, n_tok, FOX_WIDTH)
    else:
        k_past, v_past, logf_past = fox_past
        n_past = k_past.shape[1]
        k_all = jnp.concatenate([k_past.astype(k_a.dtype), k_a], axis=1)
        v_all = jnp.concatenate([v_past.astype(v_a.dtype), v_a], axis=1)
        c = jnp.cumsum(jnp.concatenate([logf_past.astype(jnp.float32), logf], axis=1), axis=1)
        k_pos = jnp.arange(n_past + n_tok)
        o_a = fox_attend(q_a, k_all, v_all, c[:, n_past:], c, k_pos[n_past:], k_pos)
        o_a = o_a.reshape(bsz, n_tok, FOX_WIDTH)
    y_a = (o_a * jax.nn.silu(gate_a)) @ w_out_fox

    xpad = jnp.concatenate([conv_buf.astype(qkv_b.dtype), qkv_b], axis=1)
    new_conv = xpad[:, -(CONV_W - 1):]
    conv = jax.nn.silu(sum(xpad[:, i:i + n_tok] * gdn_conv_w[i] for i in range(CONV_W)))
    q_b, k_b, v_b = jnp.split(conv, [GDN_KEY_WIDTH, 2 * GDN_KEY_WIDTH], axis=-1)
    q_b = l2_normalize(q_b.reshape(bsz, n_tok, GDN_HEADS, GDN_DK)) * GDN_SCALE
    k_b = l2_normalize(k_b.reshape(bsz, n_tok, GDN_HEADS, GDN_DK))
    v_b = v_b.reshape(bsz, n_tok, GDN_HEADS, GDN_DV).astype(jnp.float32)
    g = -jnp.exp(gdn_a_log.astype(jnp.float32)) * jax.nn.softplus(a_b.astype(jnp.float32) + gdn_dt_bias)
    beta = jax.nn.sigmoid(beta_b.astype(jnp.float32))
    o_b, s_new = gated_delta_chunked(q_b, k_b, v_b, g, beta, s0.astype(jnp.float32))
    o_b = rms_norm(o_b, gdn_norm_w).astype(x.dtype).reshape(bsz, n_tok, GDN_VAL_WIDTH)
    y_b = (o_b * jax.nn.silu(gate_b)) @ w_out_gdn

    h = (jax.nn.sigmoid(m_a) * y_a + jax.nn.sigmoid(m_b) * y_b) @ w_out
    x1 = layer_norm(DEEPNORM_ALPHA * x + h, ln_g, ln_b)
    e = rms_norm(p_l @ w_pl_proj, pl_norm_w)
    x_out = x1 + jax.nn.sigmoid(x1 @ w_pl_gate) * e
    return (x_out, k_a, v_a, logf.astype(x.dtype), new_conv, s_new.astype(x.dtype))


def setup_inputs(seed: int = 0) -> dict:
    key = jax.random.key(seed)
    ks = jax.random.split(key, 24)
    nrm = jax.random.normal
    f32 = jnp.float32
    col_scale = jnp.concatenate([
        jnp.ones((2 * FOX_WIDTH,), f32), jnp.full((FOX_WIDTH,), DEEPNORM_BETA, f32),
        jnp.ones((FOX_HEADS + FOX_WIDTH + 2 * GDN_KEY_WIDTH,), f32),
        jnp.full((GDN_VAL_WIDTH,), DEEPNORM_BETA, f32),
        jnp.ones((2 * GDN_HEADS + GDN_VAL_WIDTH + 2 * D_MODEL,), f32)])
    dt = jnp.exp(jax.random.uniform(ks[12], (DEPTH, GDN_HEADS), f32, np.log(1e-3), np.log(1e-1)))
    return {
        'x_prompt': nrm(ks[0], (BATCH, SEQ, D_MODEL), f32),
        'x_sample': nrm(ks[1], (DEC_BATCH, DEC_SEQ, D_MODEL), f32),
        'cache_fox_k': nrm(ks[2], (DEPTH, DEC_BATCH, PAST_LEN, FOX_HEADS, FOX_HEAD_DIM), f32),
        'cache_fox_v': nrm(ks[3], (DEPTH, DEC_BATCH, PAST_LEN, FOX_HEADS, FOX_HEAD_DIM), f32),
        'cache_fox_logf': jax.nn.log_sigmoid(3.0 + nrm(ks[4], (DEPTH, DEC_BATCH, PAST_LEN, FOX_HEADS), f32)),
        'state_gdn_conv': nrm(ks[5], (DEPTH, DEC_BATCH, CONV_W - 1, GDN_CONV_CH), f32),
        'state_gdn': 0.1 * nrm(ks[6], (DEPTH, DEC_BATCH, GDN_HEADS, GDN_DK, GDN_DV), f32),
        'p_prompt': nrm(ks[7], (DEPTH, BATCH, SEQ, PLE_DIM), f32),
        'p_sample': nrm(ks[8], (DEPTH, DEC_BATCH, DEC_SEQ, PLE_DIM), f32),
        'w_in': nrm(ks[9], (DEPTH, D_MODEL, N_IN), f32) * (D_MODEL ** -0.5) * col_scale,
        'fox_f_bias': 3.0 + 0.5 * nrm(ks[10], (DEPTH, FOX_HEADS), f32),
        'gdn_conv_w': nrm(ks[11], (DEPTH, CONV_W, GDN_CONV_CH), f32) * (CONV_W ** -0.5),
        'gdn_a_log': jnp.log(jax.random.uniform(ks[13], (DEPTH, GDN_HEADS), f32, 1.0, 16.0)),
        'gdn_dt_bias': dt + jnp.log(-jnp.expm1(-dt)),
        'gdn_norm_w': 1.0 + 0.05 * nrm(ks[14], (DEPTH, GDN_DV), f32),
        'w_out_fox': nrm(ks[15], (DEPTH, FOX_WIDTH, D_MODEL), f32) * (FOX_WIDTH ** -0.5) * DEEPNORM_BETA,
        'w_out_gdn': nrm(ks[16], (DEPTH, GDN_VAL_WIDTH, D_MODEL), f32) * (GDN_VAL_WIDTH ** -0.5) * DEEPNORM_BETA,
        'w_out': nrm(ks[17], (DEPTH, D_MODEL, D_MODEL), f32) * (D_MODEL ** -0.5) * DEEPNORM_BETA,
        'ln_g': 1.0 + 0.05 * nrm(ks[18], (DEPTH, D_MODEL), f32),
        'ln_b': 0.02 * nrm(ks[19], (DEPTH, D_MODEL), f32),
        'w_pl_proj': nrm(ks[20], (DEPTH, PLE_DIM, D_MODEL), f32) * (PLE_DIM ** -0.5),
        'w_pl_gate': nrm(ks[21], (DEPTH, D_MODEL, D_MODEL), f32) * (D_MODEL ** -0.5),
        'pl_norm_w': 1.0 + 0.05 * nrm(ks[22], (DEPTH, D_MODEL), f32),
    }


def reference(x_prompt, x_sample, cache_fox_k, cache_fox_v, cache_fox_logf, state_gdn_conv, state_gdn,
              p_prompt, p_sample, w_in, fox_f_bias, gdn_conv_w, gdn_a_log, gdn_dt_bias, gdn_norm_w,
              w_out_fox, w_out_gdn, w_out, ln_g, ln_b, w_pl_proj, w_pl_gate, pl_norm_w):
    xp, xs = x_prompt, x_sample
    bp = x_prompt.shape[0]
    kp_l, vp_l, lfp_l, cvp_l, sp_l = [], [], [], [], []
    ks_l, vs_l, lfs_l, cvs_l, ss_l = [], [], [], [], []
    for i in range(DEPTH):
        w = (w_in[i], fox_f_bias[i], gdn_conv_w[i], gdn_a_log[i], gdn_dt_bias[i], gdn_norm_w[i],
             w_out_fox[i], w_out_gdn[i], w_out[i], ln_g[i], ln_b[i], w_pl_proj[i], w_pl_gate[i], pl_norm_w[i])
        conv0 = jnp.zeros((bp, CONV_W - 1, GDN_CONV_CH), xp.dtype)
        s00 = jnp.zeros((bp, GDN_HEADS, GDN_DK, GDN_DV), xp.dtype)
        xp, kp, vp, lfp, cvp, sp = layer(xp, p_prompt[i], *w, None, conv0, s00)
        xs, kss, vss, lfs, cvs, sss = layer(xs, p_sample[i], *w,
                                            (cache_fox_k[i], cache_fox_v[i], cache_fox_logf[i]),
                                            state_gdn_conv[i], state_gdn[i])
        kp_l.append(kp); vp_l.append(vp); lfp_l.append(lfp); cvp_l.append(cvp); sp_l.append(sp)
        ks_l.append(kss); vs_l.append(vss); lfs_l.append(lfs); cvs_l.append(cvs); ss_l.append(sss)
    return (xp, xs,
            jnp.stack(kp_l), jnp.stack(vp_l), jnp.stack(lfp_l), jnp.stack(cvp_l), jnp.stack(sp_l),
            jnp.stack(ks_l), jnp.stack(vs_l), jnp.stack(lfs_l), jnp.stack(cvs_l), jnp.stack(ss_l))
```

```python
import bisect
import os
import numpy as np
import concourse.bass as bass
import concourse.mybir as mybir
from concourse.bass_utils import run_bass_kernel_spmd
from contextlib import ExitStack

F32 = mybir.dt.float32
BF16 = mybir.dt.bfloat16
AF = mybir.ActivationFunctionType
ALU = mybir.AluOpType

ENGS = ('pe', 'act', 'dve', 'pool', 'sp')


class Buf:
    __slots__ = ('name', 'lw', 'rd', 'x')

    def __init__(self, name='', x=False):
        self.name = name
        self.lw = None
        self.rd = []
        self.x = x


class Op:
    __slots__ = ('eng', 'idx', 'fn', 'deps', 'inc', 'semkey', 'cum', 'isdma', 'waits', 'semval')


class Prog:
    def __init__(self):
        self.ops = {e: [] for e in ENGS}
        self.dma_cum = {}
        self.order = []

    def op(self, eng, fn, reads=(), writes=(), dma_sem=None):
        o = Op()
        o.eng = eng; o.fn = fn; o.inc = False; o.isdma = dma_sem is not None
        o.semkey = dma_sem; o.cum = 0; o.waits = None; o.semval = 0
        xr = [b for b in reads if b.x]
        if xr:
            reads = [b for b in reads if not b.x]
            writes = list(writes) + xr
        lst = self.ops[eng]
        o.idx = len(lst)
        lst.append(o)
        self.order.append(o)
        deps = []
        for b in reads:
            if b.lw is not None:
                deps.append(b.lw)
        for b in writes:
            if b.lw is not None:
                deps.append(b.lw)
            deps.extend(b.rd)
        if eng == 'pe' and not o.isdma:
            deps = [d for d in deps if d.isdma or d.eng != 'pe']
        o.deps = deps
        if o.isdma:
            c = self.dma_cum.get(dma_sem, 0) + 16
            self.dma_cum[dma_sem] = c
            o.cum = c
        for b in reads:
            b.rd.append(o)
        for b in writes:
            b.lw = o
            b.rd = []
        return o

    def plan(self):
        known = {e: {} for e in ENGS}
        hidx = {e: [] for e in ENGS}
        hsnap = {e: [] for e in ENGS}
        nw = 0
        for o in self.order:
            F = o.eng
            kn = known[F]
            waits = []
            ds = sorted(set(o.deps), key=lambda d: -(d.cum if d.isdma else d.idx))
            for d in ds:
                if d is o:
                    continue
                if d.isdma:
                    key = ('d', d.semkey); val = d.cum
                else:
                    key = ('c', d.eng); val = d.idx
                if kn.get(key, -1) >= val:
                    continue
                waits.append(d)
                if not d.isdma:
                    d.inc = True
                kn[key] = val
                E = d.eng
                hi = hidx[E]
                p = bisect.bisect_right(hi, d.idx) - 1
                if p >= 0:
                    for k2, v2 in hsnap[E][p].items():
                        if kn.get(k2, -1) < v2:
                            kn[k2] = v2
            o.waits = waits
            if waits:
                nw += len(waits)
                hidx[F].append(o.idx)
                hsnap[F].append(dict(kn))
        for e in ENGS:
            c = 0
            for o in self.ops[e]:
                if o.inc:
                    c += 1
                o.semval = c
        self.nwaits = nw

    def emit(self, nc, es):
        self.plan()
        csem = {e: es.enter_context(nc.semaphore('c_' + e)) for e in ENGS}
        dsem = {k: es.enter_context(nc.semaphore('d_%d' % i)) for i, k in enumerate(self.dma_cum)}
        block = es.enter_context(nc.Block())
        ops = self.ops

        def run(e, eng):
            mysem = csem[e]
            for o in ops[e]:
                for d in o.waits:
                    if d.isdma:
                        eng.wait_ge(dsem[d.semkey], d.cum)
                    else:
                        eng.wait_ge(csem[d.eng], d.semval)
                if o.fn is None:
                    continue
                ins = o.fn(eng)
                if o.isdma:
                    ins.then_inc(dsem[o.semkey], 16)
                elif o.inc:
                    ins.then_inc(mysem, 1)

        @block.sync
        def _(eng):
            run('sp', eng)

        @block.tensor
        def _(eng):
            run('pe', eng)

        @block.scalar
        def _(eng):
            run('act', eng)

        @block.vector
        def _(eng):
            run('dve', eng)

        @block.gpsimd
        def _(eng):
            run('pool', eng)


D = 2048
KC = 16
H = 8
NIN = 12312
OFF_Q, OFF_K, OFF_V, OFF_F, OFF_GA = 0, 1024, 2048, 3072, 3080
OFF_QKVB, OFF_A, OFF_BETA, OFF_GB, OFF_MA, OFF_MB = 4104, 7176, 7184, 7192, 8216, 10264
SPL = 148
LN_EPS = 1e-5
RMS_EPS = 1e-6
NEG = -30000.0

C_ID, C_ONES, C_BU, C_SAMEC, C_CMA, C_CMB, C_POSLS, C_NEGU, C_NEGA, C_SEL, C_SUT = (
    0, 128, 256, 384, 512, 640, 768, 896, 1024, 1152, 2176)
NCON = 2304


def make_consts():
    c = np.zeros((128, NCON), np.float32)
    i = np.arange(128)
    same = (i[:, None] // 64) == (i[None, :] // 64)
    c[:, C_ID:C_ID + 128] = np.eye(128)
    c[:, C_ONES:C_ONES + 128] = 1.0
    c[:, C_BU:C_BU + 128] = ((i[:, None] <= i[None, :]) & same)
    c[:, C_SAMEC:C_SAMEC + 128] = same
    c[:, C_CMA:C_CMA + 128] = (i[:, None] < 64)
    c[:, C_CMB:C_CMB + 128] = (i[:, None] >= 64)
    c[:, C_POSLS:C_POSLS + 128] = np.where((i[:, None] > i[None, :]) & same, 0.0, -NEG)
    c[:, C_NEGU:C_NEGU + 128] = np.where((i[:, None] <= i[None, :]) & same, 0.0, NEG)
    c[:, C_NEGA:C_NEGA + 128] = np.where(i[:, None] <= i[None, :], 0.0, NEG)
    for h in range(8):
        c[h, C_SEL + h * 128:C_SEL + (h + 1) * 128] = 1.0
    c[:, C_SUT:C_SUT + 128] = (i[:, None] < i[None, :])
    return c


def make_smallp(inp, depth):
    sp = np.zeros((128, depth * SPL), np.float32)
    for l in range(depth):
        b = l * SPL
        sp[0:8, b + 0] = inp['fox_f_bias'][l]
        sp[32:40, b + 1] = inp['gdn_dt_bias'][l]
        sp[32:40, b + 2] = inp['gdn_a_log'][l]
        sp[:, b + 3] = inp['gdn_norm_w'][l]
        cw = inp['gdn_conv_w'][l]
        sp[:, b + 4:b + 100] = cw.reshape(4, 24, 128).transpose(2, 1, 0).reshape(128, 96)
        sp[:, b + 100:b + 116] = inp['ln_g'][l].reshape(16, 128).T
        sp[:, b + 116:b + 132] = inp['ln_b'][l].reshape(16, 128).T
        sp[:, b + 132:b + 148] = inp['pl_norm_w'][l].reshape(16, 128).T
    return sp


def build(DEPTH, SEQ, PAST, NT, DS=16, ALPHA=(2 * 4) ** 0.25):
    nc = bass.Bass('TRN2', target_bir_lowering=False)
    P = Prog()
    es = ExitStack()
    FOX_SCALE = 128 ** -0.5
    GDN_SCALE = 128 ** -0.5
    NTILES = SEQ // NT
    PS_ = PAST // 128

    def din(n, s, dt=F32):
        return nc.dram_tensor(n, list(s), dt, kind='ExternalInput').ap()

    def dout(n, s):
        return nc.dram_tensor(n, list(s), F32, kind='ExternalOutput').ap()

    def dscr(n, s, dt):
        return nc.dram_tensor(n, list(s), dt, kind='Internal').ap()

    x_d = {'p': din('xp', [SEQ, D]), 's': din('xs', [DS, D])}
    ck_d = din('ck', [DEPTH, PAST, 1024]); cv_d = din('cv', [DEPTH, PAST, 1024])
    clf_d = din('clf', [DEPTH, PAST, 8])
    scv_d = din('scv', [DEPTH, 3, 3072]); sg_d = din('sg', [DEPTH, 8, 128, 128])
    p_d = {'p': din('pp', [DEPTH, SEQ, 256]), 's': din('ps', [DEPTH, DS, 256])}
    WSPEC = (('in', 96, 2048), ('of', 16, 1024), ('og', 16, 1024), ('o', 16, 2048), ('pp', 16, 256), ('pg', 16, 2048))
    wt_d = {k: din('wt_' + k, [DEPTH, nt_, 128, wd_]) for k, nt_, wd_ in WSPEC}
    ws_d = {k: dscr('ws_' + k, [DEPTH, nt_, 128, wd_], BF16) for k, nt_, wd_ in WSPEC}
    wsm_d = din('wsm_d', [DEPTH, 128, 16 * 96])
    wscr = [Buf('wscr%d' % l) for l in range(DEPTH)]
    smallp_d = din('smallp', [128, DEPTH * SPL]); consts_d = din('consts', [128, NCON])
    y_d = {'p': dout('yp', [SEQ, D]), 's': dout('ys', [DS, D])}
    k_o = {'p': dout('kp', [DEPTH, SEQ, 1024]), 's': dout('ks', [DEPTH, DS, 1024])}
    v_o = {'p': dout('vp', [DEPTH, SEQ, 1024]), 's': dout('vs', [DEPTH, DS, 1024])}
    lf_o = {'p': dout('lfp', [DEPTH, SEQ, 8]), 's': dout('lfs', [DEPTH, DS, 8])}
    cv_o = {'p': dout('cvp', [DEPTH, 3, 3072]), 's': dout('cvs', [DEPTH, 3, 3072])}
    st_o = {'p': dout('stp', [DEPTH, 8, 128, 128]), 's': dout('sts', [DEPTH, 8, 128, 128])}
    kT_scr = dscr('kTscr', [DEPTH, 8, 128, SEQ], BF16)
    v_scr = dscr('vscr', [DEPTH, 8, 128, SEQ // 128, 128], BF16)
    OUTB = Buf('outputs')

    class T:
        def __init__(s, name, shape, dt, nb=0):
            s.h = es.enter_context(nc.sbuf_tensor(name, list(shape), dt))
            s.b = Buf(name)
            s.bs = [Buf('%s_%d' % (name, i)) for i in range(nb)]
            s.name = name

        def __getitem__(s, k):
            return s.h[k]

    class RPool:
        def __init__(s, name, shape, dt, n):
            s.ts = [T('%s%d' % (name, i), shape, dt) for i in range(n)]
            s.i = 0

        def get(s):
            t = s.ts[s.i % len(s.ts)]
            s.i += 1
            return t

    class PS:
        def __init__(s, name):
            s.h = es.enter_context(nc.psum_tensor(name, [128, 512], F32))
            s.b = Buf(name, x=True)

        def __getitem__(s, k):
            return s.h[k]

    psall = [PS('ps%d' % i) for i in range(8)]
    ps_acc0, ps_acc1 = psall[0], psall[1]

    class PSPool:
        def __init__(s):
            s.i = 0

        def get(s):
            t = psall[2 + s.i % 6]
            s.i += 1
            return t
    psum = PSPool()

    def bl(xs):
        out = []
        for x in xs:
            if isinstance(x, Buf):
                out.append(x)
            elif isinstance(x, (list, tuple)):
                out.extend(bl(x))
            else:
                out.append(x.b)
        return out

    def MM(out, lhsT, rhs, R, W, start=True, stop=True):
        P.op('pe', lambda e: e.matmul(out, lhsT, rhs, start=start, stop=stop), bl(R), bl(W))

    def TR(out, in_, ident, R, W):
        P.op('pe', lambda e: e.transpose(out, in_, ident), bl(R), bl(W))

    def ACT(out, in_, func, R, W, bias=None, scale=None):
        kw = {}
        if bias is not None:
            kw['bias'] = bias
        if scale is not None:
            kw['scale'] = scale
        P.op('act', lambda e: e.activation(out=out, in_=in_, func=func, **kw), bl(R), bl(W))

    def TTo(out, in0, in1, op, R, W, eng='dve'):
        P.op(eng, lambda e: e.tensor_tensor(out=out, in0=in0, in1=in1, op=op), bl(R), bl(W))

    def TS(out, in0, s1, op0, R, W, s2=None, op1=None, eng='dve'):
        if op1 is None:
            P.op(eng, lambda e: e.tensor_scalar(out=out, in0=in0, scalar1=s1, scalar2=None, op0=op0), bl(R), bl(W))
        else:
            P.op(eng, lambda e: e.tensor_scalar(out=out, in0=in0, scalar1=s1, scalar2=s2, op0=op0, op1=op1),
                 bl(R), bl(W))

    def STT(out, in0, scalar, in1, op0, op1, R, W):
        P.op('dve', lambda e: e.scalar_tensor_tensor(out=out, in0=in0, scalar=scalar, in1=in1, op0=op0, op1=op1),
             bl(R), bl(W))

    def CP(out, in_, R, W, eng='dve'):
        if eng == 'act':
            P.op('act', lambda e: e.copy(out=out, in_=in_), bl(R), bl(W))
        else:
            P.op(eng, lambda e: e.tensor_scalar(out=out, in0=in_, scalar1=1.0, scalar2=None, op0=ALU.mult),
                 bl(R), bl(W))

    def RECIP(out, in_, R, W):
        P.op('dve', lambda e: e.reciprocal(out=out, in_=in_), bl(R), bl(W))

    def MEMSET(t_ap, val, W, eng='dve'):
        P.op(eng, lambda e: e.memset(t_ap, val), [], bl(W))

    def DMA(q, out, in_, R, W, sem):
        P.op(q, lambda e: e.dma_start(out=out, in_=in_), bl(R), bl(W), dma_sem=sem + ('_sw' if q == 'pool' else ''))

    cst = T('cst', [128, NCON], F32)
    smp = T('smp', [128, DEPTH * SPL], F32)
    nsm = T('nsm', [128, DEPTH * SPL], F32)
    nexpA = T('nexpA', [128, DEPTH], F32)
    ones_bf = T('ones_bf', [128, 128], BF16)
    onesN = T('onesN', [128, max(NT, 128)], F32)
    negA_rep = T('negA_rep', [128, max(NT, 128)], F32)
    DMA('sp', cst[:, :], consts_d, [], [cst], 'cst')
    DMA('sp', smp[:, :], smallp_d, [], [smp], 'smp')
    CP(ones_bf[:, :], cst[:, C_ONES:C_ONES + 128], [cst], [ones_bf])
    MEMSET(onesN[:, :], 1.0, [onesN])
    for i in range(max(NT, 128) // 128):
        CP(negA_rep[:, i * 128:(i + 1) * 128], cst[:, C_NEGA:C_NEGA + 128], [cst], [negA_rep])
    TS(nsm[:, :], smp[:, :], -1.0, ALU.mult, [smp], [nsm])
    for l in range(DEPTH):
        ACT(nexpA[:, l:l + 1], smp[:, l * SPL + 2:l * SPL + 3], AF.Exp, [smp], [nexpA])
    TS(nexpA[:, :], nexpA[:, :], -1.0, ALU.mult, [nexpA], [nexpA])
    ident = lambda a, b_=None: cst[0:a, C_ID:C_ID + (a if b_ is None else b_)]

    wpool = RPool('w', [128, 16, 128], BF16, 4)
    wsm = T('wsm', [128, 16, 96], BF16)
    f2k = RPool('f2k', [128, NT], F32, 4)
    b1k = RPool('b1k', [128, NT], BF16, 3)
    tpool = RPool('tp', [128, NT], F32, 2)
    ppool = RPool('pp', [128, NT], BF16, 2)
    xppool = RPool('xpp', [128, NT + 3], F32, 2)
    m32 = RPool('m32', [128, 128], F32, 6)
    mbf = RPool('mbf', [128, 128], BF16, 8)
    gbf = RPool('gbf', [128, 128], BF16, 6)
    kst = RPool('kst', [128, NT // 128, 128], F32, 2)
    vst = RPool('vst', [128, NT // 128, 128], F32, 2)
    NKT = max(SEQ, PAST + 128)
    KTp = RPool('KT', [128, NKT], BF16, 2)
    VTp = RPool('VT', [128, NKT // 128, 128], BF16, 2)
    ktok32 = T('ktok32', [128, max(PS_, 24), 128], F32)
    class View:
        def __init__(s, base_ap, b, name):
            s.ap = base_ap
            s.b = b
            s.name = name

        def __getitem__(s, k):
            return s.ap[k]
    cvst = View(ktok32.h[0:3, 0:24, :].rearrange("p a b -> p (a b)"), ktok32.b, 'cvst')
    xtok = View(ktok32.h[:, 0:16, :].rearrange("p a b -> p (a b)"), ktok32.b, 'xtok')
    Sb = T('Sb', [128, 8, 128], BF16, nb=8)
    kscrB = [[Buf('kscr%d_%d' % (l, h)) for h in range(8)] for l in range(DEPTH)]

    def wload(key, l, ti, kc=16):
        t = wpool.get()
        DMA('pool', t[:, 0:kc, :], ws_d[key][l, ti].rearrange("p (k c) -> p k c", k=kc), [wscr[l]], [t], t.name)
        return t

    pre_done = set()

    def prepass(l):
        if l >= DEPTH or l in pre_done:
            return
        pre_done.add(l)
        for key, nt_, wd_ in WSPEC:
            tot = nt_ * wd_
            src = wt_d[key][l].rearrange("t p c -> (t p) c").rearrange("(a b) c -> a (b c)", a=128)
            dst = ws_d[key][l].rearrange("t p c -> (t p) c").rearrange("(a b) c -> a (b c)", a=128)
            step = 8192
            for c0 in range(0, tot, step):
                c1 = min(tot, c0 + step)
                DMA('pool', dst[:, c0:c1], src[:, c0:c1], [], [wscr[l]], 'pre%d' % l)

    class KB:
        pass
    KBs = {}
    for kind, n in (('p', NT), ('s', DS)):
        B = KB()
        B.n = n
        B.sp = min(n, 128)
        B.nsub = max(1, n // 128)
        B.C = 64 if kind == 'p' else 16
        B.xT32 = T(kind + 'xT32', [128, 16, n], F32, nb=16)
        B.xTb = T(kind + 'xTb', [128, 16, n], BF16, nb=16)
        B.gaT = T(kind + 'gaT', [128, 8, n], BF16, nb=8)
        B.gbT = T(kind + 'gbT', [128, 8, n], BF16, nb=8)
        B.mT = T(kind + 'mT', [128, 16, n], BF16, nb=16)
        B.smT = T(kind + 'smT', [128, n], F32)
        B.scT = T(kind + 'scT', [128, n], F32)
        B.smtok = T(kind + 'smtok', [128, B.nsub, 128], F32)
        B.sctok = T(kind + 'sctok', [128, B.nsub, 128], F32)
        B.carry = T(kind + 'carry', [128, DEPTH], F32)
        nsa = (SEQ // 128) if kind == 'p' else 1
        B.ncall = T(kind + 'ncall', [128, DEPTH, nsa, 8], F32)
        B.gctok = T(kind + 'gctok', [128, B.nsub, 8], F32)
        B.ngctok = T(kind + 'ngctok', [128, B.nsub, 8], F32)
        B.gcT = T(kind + 'gcT', [8, n], F32)
        B.bege = T(kind + 'bege', [128, B.nsub, 8], F32)
        B.nbeta = T(kind + 'nbeta', [128, B.nsub, 8], F32)
        B.tail = T(kind + 'tail', [128, B.nsub, 8], F32)
        B.glb = T(kind + 'glb', [128, B.nsub, 2, 8], F32)
        B.SD = DEPTH if kind == 'p' else 1
        B.S = T(kind + 'S', [128, B.SD, 8, 128], F32, nb=B.SD * 8)
        B.convc = T(kind + 'convc', [128, DEPTH, 24, 3], F32)
        B.cqb = T(kind + 'cqb', [128, n], F32)
        B.cqm = T(kind + 'cqm', [128, n], F32)
        B.qT = T(kind + 'qT', [128, n], BF16)
        B.sga = T(kind + 'sga', [128, n], F32)
        B.kT32 = T(kind + 'kT32', [128, n], F32)
        B.vT32 = T(kind + 'vT32', [128, n], F32)
        B.qTn = T(kind + 'qTn', [128, n], BF16)
        B.kTn = T(kind + 'kTn', [128, n], BF16)
        B.kbg = T(kind + 'kbg', [128, B.nsub, 128], BF16)
        B.ktl = T(kind + 'ktl', [128, B.nsub, 128], BF16)
        B.vb = T(kind + 'vb', [128, B.nsub, 128], BF16)
        B.ptok = T(kind + 'ptok', [128, B.nsub, 256], F32)
        B.pTb = T(kind + 'pTb', [128, 2, n], BF16)
        B.stat = [T(kind + 'stat%d' % i, [128, n], F32) for i in range(3)]
        MEMSET(B.smT[:, :], 0.0, [B.smT])
        MEMSET(B.scT[:, :], 0.0, [B.scT])
        MEMSET(B.carry[:, :], 0.0, [B.carry])
        KBs[kind] = B
    DBG = None
    if os.environ.get('MK_DEBUG'):
        DBG = {'p': T('dbg_p', [128, 8, NT], F32), 's': T('dbg_s', [128, 8, DS], F32)}
    lfp = T('lfpast', [128, PS_, 8], F32)
    cpast = T('cpast', [128, 8, PS_], F32)
    ncpast = T('ncpast', [128, 8, PS_], F32)
    pre = T('pre', [128, 8], F32)
    Bp = KBs['p']
    MEMSET(Bp.S[:, :, :, :], 0.0, Bp.S.bs)
    MEMSET(Bp.convc[:, :, :, :], 0.0, [Bp.convc])

    def proj_fm(B, wt, src, kc=16, mcols=128):
        n = B.n
        ps = psum.get()
        for k in range(kc):
            MM(ps[0:mcols, 0:n], wt[:, k, 0:mcols], src[:, k, 0:n], [wt, src.bs[k]], [ps],
               start=(k == 0), stop=(k == kc - 1))
        return ps

    def load_tile(kind, t0):
        B = KBs[kind]
        n, sp, nsub = B.n, B.sp, B.nsub
        for sub in range(nsub):
            DMA('sp', xtok[0:sp, :], x_d[kind][t0 + sub * 128:t0 + sub * 128 + sp, :], [], [xtok], 'xtok')
            for kg in range(4):
                ps = psum.get()
                for j in range(4):
                    k = kg * 4 + j
                    TR(ps[:, j * 128:j * 128 + sp], xtok[0:sp, k * 128:(k + 1) * 128], ident(sp), [xtok, cst], [ps])
                src = ps[:, :].rearrange("p (j t) -> p j t", j=4)[:, :, 0:sp]
                ACT(B.xT32[:, kg * 4:(kg + 1) * 4, sub * 128:sub * 128 + sp], src, AF.Copy, [ps],
                    B.xT32.bs[kg * 4:(kg + 1) * 4])
                if os.environ.get('MK_NODVE') is None:
                    CP(B.xTb[:, kg * 4:(kg + 1) * 4, sub * 128:sub * 128 + sp], src, [ps], B.xTb.bs[kg * 4:(kg + 1) * 4])

    def store_tile(kind, t0):
        B = KBs[kind]
        n, sp, nsub = B.n, B.sp, B.nsub
        for sub in range(nsub):
            for kg in range(4):
                ps = psum.get()
                for j in range(4):
                    k = kg * 4 + j
                    TR(ps[0:sp, j * 128:(j + 1) * 128], B.xT32[:, k, sub * 128:sub * 128 + sp], ident(128),
                       [B.xT32.bs[k], cst], [ps])
                if kg % 2 == 0:
                    ACT(xtok[0:sp, kg * 512:(kg + 1) * 512], ps[0:sp, :], AF.Copy, [ps], [xtok])
                else:
                    CP(xtok[0:sp, kg * 512:(kg + 1) * 512], ps[0:sp, :], [ps], [xtok])
            DMA('sp', y_d[kind][t0 + sub * 128:t0 + sub * 128 + sp, :], xtok[0:sp, :], [xtok, OUTB], [], 'xtok_st')

    STOP = float(os.environ.get('MK_STOP', '99'))

    def layer(kind, l, t0, last_tile):
        B = KBs[kind]
        n, sp, nsub, C = B.n, B.sp, B.nsub, B.C
        sb0 = l * SPL
        ls = l if kind == 'p' else 0
        gsub0 = t0 // 128 if kind == 'p' else 0
        prepass(l)

        DMA('pool', wsm[:, :, :], wsm_d[l].rearrange("p (k c) -> p k c", k=16), [], [wsm], 'wsm')
        if STOP <= 1.2:
            return
        if kind == 's':
            DMA('sp', lfp[:, :, :], clf_d[l].rearrange("(p s) h -> p s h", s=PS_), [], [lfp], 'lfp')
            for h in range(8):
                P.op('dve', lambda e, h=h: e.tensor_tensor_scan(out=cpast[:, h, :], data0=onesN[:, 0:PS_],
                                                                 data1=lfp[:, :, h], initial=0.0,
                                                                 op0=ALU.mult, op1=ALU.add), bl([lfp, onesN]), bl([cpast]))
            ps = psum.get()
            MM(ps[:, 0:8], cst[:, C_SUT:C_SUT + 128], cpast[:, :, PS_ - 1], [cst, cpast], [ps])
            MM(ps[0:8, 16:17], cpast[:, :, PS_ - 1], cst[:, C_ONES:C_ONES + 1], [cst, cpast], [ps])
            CP(pre[:, :], ps[:, 0:8], [ps], [pre])
            CP(B.carry[0:8, l:l + 1], ps[0:8, 16:17], [ps], [B.carry])
            for h in range(8):
                TS(cpast[:, h, :], cpast[:, h, :], pre[:, h:h + 1], ALU.add, [cpast, pre], [cpast])
            TS(ncpast[:, :, :], cpast[:, :, :], -1.0, ALU.mult, [cpast], [ncpast])
            DMA('sp', B.S[:, ls, :, :], sg_d[l].rearrange("h k v -> k h v"), [], B.S.bs[ls * 8:(ls + 1) * 8], 'sgl')
            DMA('sp', cvst[:, :], scv_d[l], [], [cvst], 'cvst_l')
            for cid in range(24):
                ps = psum.get()
                TR(ps[:, 0:3], cvst[0:3, cid * 128:(cid + 1) * 128], ident(3), [cvst, cst], [ps])
                CP(B.convc[:, l, cid, :], ps[:, 0:3], [ps], [B.convc])

        if STOP <= 1.4:
            return
        ps = proj_fm(B, wsm, B.xTb, 16, 96)
        ACT(B.smT[0:8, 0:n], ps[0:8, 0:n], AF.Exp, [ps, nsm], [B.smT], bias=nsm[0:8, sb0:sb0 + 1], scale=-1.0)
        ACT(B.smT[0:8, 0:n], B.smT[0:8, 0:n], AF.Ln, [B.smT], [B.smT], bias=1.0)
        TS(B.smT[0:8, 0:n], B.smT[0:8, 0:n], -1.0, ALU.mult, [B.smT], [B.smT])
        ACT(B.smT[32:40, 0:n], ps[32:40, 0:n], AF.Exp, [ps, smp], [B.smT], bias=smp[32:40, sb0 + 1:sb0 + 2])
        ACT(B.smT[32:40, 0:n], B.smT[32:40, 0:n], AF.Ln, [B.smT], [B.smT], bias=1.0)
        TS(B.smT[32:40, 0:n], B.smT[32:40, 0:n], nexpA[32:40, l:l + 1], ALU.mult, [B.smT, nexpA], [B.smT])
        ACT(B.smT[64:72, 0:n], ps[64:72, 0:n], AF.Sigmoid, [ps], [B.smT])
        if STOP <= 1.6:
            return
        P.op('dve', lambda e: e.tensor_tensor_scan(out=B.scT[:, 0:n], data0=onesN[:, 0:n], data1=B.smT[:, 0:n],
                                                   initial=B.carry[:, l:l + 1], op0=ALU.mult, op1=ALU.add),
             bl([B.smT, onesN, B.carry]), bl([B.scT]))
        CP(B.carry[:, l:l + 1], B.scT[:, n - 1:n], [B.scT], [B.carry])
        if STOP <= 1.65:
            return
        for sub in range(nsub):
            cs = slice(sub * 128, sub * 128 + sp)
            ps = psum.get()
            TR(ps[0:sp, 0:128], B.smT[:, cs], ident(128), [B.smT, cst], [ps])
            if STOP > 1.7:
                TR(ps[0:sp, 128:256], B.scT[:, cs], ident(128), [B.scT, cst], [ps])
            if STOP > 1.72:
                ACT(B.smtok[0:sp, sub, :], ps[0:sp, 0:128], AF.Copy, [ps], [B.smtok])
            if STOP > 1.74:
                ACT(B.sctok[0:sp, sub, :], ps[0:sp, 128:256], AF.Copy, [ps], [B.sctok])
        if STOP <= 1.8:
            return
        TS(B.ncall[0:sp, l, gsub0:gsub0 + nsub, :], B.sctok[0:sp, :, 0:8], -1.0, ALU.mult, [B.sctok], [B.ncall])
        DMA('sp', lf_o[kind][l, t0:t0 + n, :].rearrange("(s p) h -> p s h", p=sp), B.smtok[0:sp, :, 0:8],
            [B.smtok, OUTB], [], kind + 'lf_st')
        for sub in range(nsub):
            g_tok = B.smtok[0:sp, sub, 32:40]
            ps = psum.get()
            MM(ps[0:sp, 0:8], cst[0:sp, C_BU:C_BU + sp], g_tok, [cst, B.smtok], [ps])
            MM(ps[0:8, 128:128 + sp], g_tok, cst[0:sp, C_BU:C_BU + sp], [cst, B.smtok], [ps])
            MM(ps[0:sp, 256:264], cst[0:sp, C_SAMEC:C_SAMEC + sp], g_tok, [cst, B.smtok], [ps])
            MM(ps[:, 272:280], cst[0:sp, C_CMA:C_CMA + 128], g_tok, [cst, B.smtok], [ps])
            MM(ps[:, 288:296], cst[0:sp, C_CMB:C_CMB + 128], g_tok, [cst, B.smtok], [ps])
            CP(B.gctok[0:sp, sub, :], ps[0:sp, 0:8], [ps], [B.gctok])
            TS(B.ngctok[0:sp, sub, :], ps[0:sp, 0:8], -1.0, ALU.mult, [ps], [B.ngctok])
            CP(B.gcT[0:8, sub * 128:sub * 128 + sp], ps[0:8, 128:128 + sp], [ps], [B.gcT])
            ACT(B.bege[0:sp, sub, :], ps[0:sp, 0:8], AF.Exp, [ps], [B.bege])
            TTo(B.tail[0:sp, sub, :], ps[0:sp, 256:264], B.gctok[0:sp, sub, :], ALU.subtract, [ps, B.gctok], [B.tail])
            ACT(B.tail[0:sp, sub, :], B.tail[0:sp, sub, :], AF.Exp, [B.tail], [B.tail])
            ACT(B.glb[:, sub, 0, :], ps[:, 272:280], AF.Exp, [ps], [B.glb])
            ACT(B.glb[:, sub, 1, :], ps[:, 288:296], AF.Exp, [ps], [B.glb])
            TTo(B.bege[0:sp, sub, :], B.bege[0:sp, sub, :], B.smtok[0:sp, sub, 64:72], ALU.mult,
                [B.bege, B.smtok], [B.bege])
            TS(B.nbeta[0:sp, sub, :], B.smtok[0:sp, sub, 64:72], -1.0, ALU.mult, [B.smtok], [B.nbeta])

        if STOP <= 2:
            return
        for h in range(8):
            KT = KTp.get()
            VT = VTp.get()
            if kind == 'p':
                kbase = t0
                if t0 > 0:
                    DMA('sp', KT[:, 0:t0], kT_scr[l, h][:, 0:t0], [kscrB[l][h]], [KT], KT.name + 'ld')
                    DMA('sp', VT[:, 0:t0 // 128, :], v_scr[l, h][:, 0:t0 // 128, :],
                        [kscrB[l][h]], [VT], VT.name + 'ld')
            else:
                kbase = PAST
                DMA('sp', ktok32[:, 0:PS_, :], ck_d[l][:, h * 128:(h + 1) * 128].rearrange("(p s) d -> p s d", s=PS_),
                    [], [ktok32], 'ktok32')
                DMA('pool', VT[:, 0:PS_, :], cv_d[l][:, h * 128:(h + 1) * 128].rearrange("(p s) d -> p s d", s=PS_),
                    [], [VT], VT.name + 'ld')
                for sg in range(PS_ // 4):
                    ps = psum.get()
                    for j in range(4):
                        TR(ps[:, j * 128:(j + 1) * 128], ktok32[:, sg * 4 + j, :], ident(128), [ktok32, cst], [ps])
                    if sg % 2 == 0:
                        ACT(KT[:, sg * 512:(sg + 1) * 512], ps[:, :], AF.Copy, [ps], [KT])
                    else:
                        CP(KT[:, sg * 512:(sg + 1) * 512], ps[:, :], [ps], [KT])
            wq = wload('in', l, 0 + h)
            wk = wload('in', l, 8 + h)
            wv = wload('in', l, 16 + h)
            wg = wload('in', l, 24 + h)
            ps = proj_fm(B, wq, B.xTb)
            ACT(B.qT[:, 0:n], ps[:, 0:n], AF.Copy, [ps], [B.qT])
            ps = proj_fm(B, wk, B.xTb)
            ACT(B.kT32[:, 0:n], ps[:, 0:n], AF.Copy, [ps], [B.kT32])
            CP(KT[:, kbase:kbase + n], ps[:, 0:n], [ps], [KT])
            ps = proj_fm(B, wv, B.xTb)
            ACT(B.vT32[:, 0:n], ps[:, 0:n], AF.Copy, [ps], [B.vT32])
            ps = proj_fm(B, wg, B.xTb)
            ACT(B.sga[:, 0:n], ps[:, 0:n], AF.Silu, [ps], [B.sga])
            ks_, vs_ = kst.get(), vst.get()
            vb0 = kbase // 128
            for sub in range(nsub):
                cs = slice(sub * 128, sub * 128 + sp)
                ps = psum.get()
                TR(ps[0:sp, 0:128], B.kT32[:, cs], ident(128), [B.kT32, cst], [ps])
                TR(ps[0:sp, 128:256], B.vT32[:, cs], ident(128), [B.vT32, cst], [ps])
                ACT(ks_[0:sp, sub, :], ps[0:sp, 0:128], AF.Copy, [ps], [ks_])
                ACT(vs_[0:sp, sub, :], ps[0:sp, 128:256], AF.Copy, [ps], [vs_])
                CP(VT[0:sp, vb0 + sub, :], ps[0:sp, 128:256], [ps], [VT])
            DMA('sp', k_o[kind][l, t0:t0 + n, h * 128:(h + 1) * 128].rearrange("(s p) d -> p s d", p=sp),
                ks_[0:sp, 0:nsub, :], [ks_, OUTB], [], ks_.name + 'st')
            DMA('sp', v_o[kind][l, t0:t0 + n, h * 128:(h + 1) * 128].rearrange("(s p) d -> p s d", p=sp),
                vs_[0:sp, 0:nsub, :], [vs_, OUTB], [], vs_.name + 'st')
            if kind == 'p' and not last_tile:
                DMA('sp', kT_scr[l, h][:, t0:t0 + n], KT[:, t0:t0 + n], [KT], [kscrB[l][h]], KT.name + 'st')
                DMA('sp', v_scr[l, h][:, vb0:vb0 + nsub, :], VT[:, vb0:vb0 + nsub, :],
                    [VT], [kscrB[l][h]], VT.name + 'st')
            ps = psum.get()
            MM(ps[:, 0:n], cst[0:8, C_SEL + h * 128:C_SEL + (h + 1) * 128], B.scT[0:8, 0:n], [cst, B.scT], [ps])
            ACT(B.cqb[:, 0:n], ps[:, 0:n], AF.Copy, [ps], [B.cqb])
            if kind == 'p':
                TTo(B.cqm[:, 0:n], ps[:, 0:n], negA_rep[:, 0:n], ALU.add, [ps, negA_rep], [B.cqm])
            else:
                TTo(B.cqm[0:sp, 0:n], ps[0:sp, 0:n], negA_rep[0:sp, 0:n], ALU.add, [ps, negA_rep], [B.cqm])
            kts = []
            if kind == 'p':
                for kt in range((t0 + n) // 128):
                    i = kt - t0 // 128
                    kts.append((kt * 128, 128, kt, 0 if i < 0 else i * 128, i >= 0, B.ncall[:, l, kt, h:h + 1], B.ncall))
            else:
                for s_ in range(PS_):
                    kts.append((s_ * 128, 128, s_, 0, False, ncpast[:, h, s_:s_ + 1], ncpast))
                kts.append((PAST, sp, PS_, 0, True, B.ncall[0:sp, l, 0, h:h + 1], B.ncall))
            po, pd = ps_acc0, ps_acc1
            for ii, (kc0, np_, vti, qlo, diag, nck, nckT) in enumerate(kts):
                first, lastk = ii == 0, ii == len(kts) - 1
                pss = psum.get()
                MM(pss[0:np_, qlo:n], KT[:, kc0:kc0 + np_], B.qT[:, qlo:n], [KT, B.qT], [pss])
                t = tpool.get()
                if diag:
                    w_ = min(128, n - qlo)
                    STT(t[0:np_, qlo:qlo + w_], pss[0:np_, qlo:qlo + w_], FOX_SCALE, B.cqm[0:np_, qlo:qlo + w_],
                        ALU.mult, ALU.add, [pss, B.cqm], [t])
                    if qlo + w_ < n:
                        STT(t[0:np_, qlo + w_:n], pss[0:np_, qlo + w_:n], FOX_SCALE, B.cqb[0:np_, qlo + w_:n],
                            ALU.mult, ALU.add, [pss, B.cqb], [t])
                else:
                    STT(t[0:np_, 0:n], pss[0:np_, 0:n], FOX_SCALE, B.cqb[0:np_, 0:n], ALU.mult, ALU.add,
                        [pss, B.cqb], [t])
                pT = ppool.get()
                ACT(pT[0:np_, qlo:n], t[0:np_, qlo:n], AF.Exp, [t, nckT], [pT], bias=nck)
                MM(po[:, qlo:n], VT[0:np_, vti, :], pT[0:np_, qlo:n], [VT, pT], [po], start=first, stop=lastk)
                MM(pd[:, qlo:n], ones_bf[0:np_, :], pT[0:np_, qlo:n], [ones_bf, pT], [pd], start=first, stop=lastk)
            rden = f2k.get()
            RECIP(rden[:, 0:n], pd[:, 0:n], [pd], [rden])
            t2 = f2k.get()
            TTo(t2[:, 0:n], po[:, 0:n], rden[:, 0:n], ALU.mult, [po, rden], [t2])
            TTo(B.gaT[:, h, 0:n], t2[:, 0:n], B.sga[:, 0:n], ALU.mult, [t2, B.sga], [B.gaT.bs[h]])

        if STOP <= 3:
            return
        emit_conv_out = (kind == 's') or last_tile
        for h in range(8):
            wts = [wload('in', l, 32 + j * 8 + h) for j in range(3)]
            wgb = wload('in', l, 56 + h)
            post = []
            for j in range(3):
                cid = j * 8 + h
                ps = proj_fm(B, wts[j], B.xTb)
                xp = xppool.get()
                CP(xp[:, 0:3], B.convc[:, l, cid, :], [B.convc], [xp])
                ACT(xp[:, 3:3 + n], ps[:, 0:n], AF.Copy, [ps], [xp])
                CP(B.convc[:, l, cid, :], xp[:, n:n + 3], [xp], [B.convc])
                if emit_conv_out:
                    ps2 = psum.get()
                    TR(ps2[0:3, 0:128], xp[:, n:n + 3], ident(128), [xp, cst], [ps2])
                    CP(cvst[0:3, cid * 128:(cid + 1) * 128], ps2[0:3, 0:128], [ps2], [cvst])
                acc = f2k.get()
                cw = lambda i: smp[:, sb0 + 4 + cid * 4 + i:sb0 + 4 + cid * 4 + i + 1]
                TS(acc[:, 0:n], xp[:, 0:n], cw(0), ALU.mult, [xp, smp], [acc])
                for i in range(1, 4):
                    STT(acc[:, 0:n], xp[:, i:i + n], cw(i), acc[:, 0:n], ALU.mult, ALU.add, [xp, smp, acc], [acc])
                ACT(acc[:, 0:n], acc[:, 0:n], AF.Silu, [acc], [acc])
                post.append(acc)
            aq, ak, av = post
            kT32g = f2k.get()
            for (a_, scl, outs) in ((aq, GDN_SCALE, 'q'), (ak, 1.0, 'k')):
                sq = b1k.get()
                ACT(sq[:, 0:n], a_[:, 0:n], AF.Square, [a_], [sq])
                ps = psum.get()
                MM(ps[:, 0:n], ones_bf[:, :], sq[:, 0:n], [ones_bf, sq], [ps])
                r = B.stat[0]
                ACT(r[:, 0:n], ps[:, 0:n], AF.Sqrt, [ps], [r], bias=RMS_EPS)
                RECIP(r[:, 0:n], r[:, 0:n], [r], [r])
                if outs == 'q':
                    STT(B.qTn[:, 0:n], a_[:, 0:n], scl, r[:, 0:n], ALU.mult, ALU.mult, [a_, r], [B.qTn])
                else:
                    TTo(kT32g[:, 0:n], a_[:, 0:n], r[:, 0:n], ALU.mult, [a_, r], [kT32g])
                    ACT(B.kTn[:, 0:n], kT32g[:, 0:n], AF.Copy, [kT32g], [B.kTn])
            ps = proj_fm(B, wgb, B.xTb)
            sgb = f2k.get()
            ACT(sgb[:, 0:n], ps[:, 0:n], AF.Silu, [ps], [sgb])
            for sub in range(nsub):
                cs = slice(sub * 128, sub * 128 + sp)
                ps = psum.get()
                TR(ps[0:sp, 0:128], kT32g[:, cs], ident(128), [kT32g, cst], [ps])
                TR(ps[0:sp, 128:256], av[:, cs], ident(128), [av, cst], [ps])
                TS(B.kbg[0:sp, sub, :], ps[0:sp, 0:128], B.bege[0:sp, sub, h:h + 1], ALU.mult, [ps, B.bege], [B.kbg])
                TS(B.ktl[0:sp, sub, :], ps[0:sp, 0:128], B.tail[0:sp, sub, h:h + 1], ALU.mult, [ps, B.tail], [B.ktl])
                TS(B.vb[0:sp, sub, :], ps[0:sp, 128:256], B.smtok[0:sp, sub, 64 + h:65 + h], ALU.mult,
                   [ps, B.smtok], [B.vb])
            Sbuf = B.S.bs[ls * 8 + h]
            ACT(Sb[:, h, :], B.S[:, ls, h, :], AF.Copy, [Sbuf], [Sb.bs[h]])
            po = ps_acc0
            for sub in range(nsub):
                cs = slice(sub * 128, sub * 128 + sp)
                psb = psum.get()
                MM(psb[:, 0:sp], cst[0:8, C_SEL + h * 128:C_SEL + (h + 1) * 128], B.gcT[0:8, cs], [cst, B.gcT], [psb])
                MM(psb[0:sp, 128:128 + sp], B.kTn[:, cs], B.kTn[:, cs], [B.kTn], [psb])
                MM(psb[0:sp, 256:256 + sp], B.kTn[:, cs], B.qTn[:, cs], [B.kTn, B.qTn], [psb])
                tL = m32.get()
                TTo(tL[0:sp, 0:sp], psb[0:sp, 0:sp], cst[0:sp, C_POSLS:C_POSLS + sp], ALU.add, [psb, cst], [tL])
                ACT(tL[0:sp, 0:sp], tL[0:sp, 0:sp], AF.Exp, [tL, B.gctok], [tL], bias=B.gctok[0:sp, sub, h:h + 1], scale=-1.0)
                tU = m32.get()
                TTo(tU[0:sp, 0:sp], psb[0:sp, 0:sp], cst[0:sp, C_NEGU:C_NEGU + sp], ALU.add, [psb, cst], [tU])
                ACT(tU[0:sp, 0:sp], tU[0:sp, 0:sp], AF.Exp, [tU, B.ngctok], [tU], bias=B.ngctok[0:sp, sub, h:h + 1])
                egb = m32.get()
                ACT(egb[:, 0:sp], psb[:, 0:sp], AF.Exp, [psb], [egb])
                V32 = m32.get()
                STT(V32[0:sp, 0:sp], psb[0:sp, 128:128 + sp], B.nbeta[0:sp, sub, h:h + 1], tL[0:sp, 0:sp],
                    ALU.mult, ALU.mult, [psb, B.nbeta, tL], [V32])
                Vb = mbf.get()
                ACT(Vb[0:sp, 0:sp], V32[0:sp, 0:sp], AF.Copy, [V32], [Vb])
                QKTb = gbf.get()
                TTo(QKTb[0:sp, 0:sp], psb[0:sp, 256:256 + sp], tU[0:sp, 0:sp], ALU.mult, [psb, tU], [QKTb])
                qdT = gbf.get()
                TTo(qdT[:, 0:sp], B.qTn[:, cs], egb[:, 0:sp], ALU.mult, [B.qTn, egb], [qdT])
                psu = psum.get()
                TR(psu[0:sp, 0:sp], V32[0:sp, 0:sp], ident(sp), [V32, cst], [psu])
                Ub = mbf.get()
                ACT(Ub[0:sp, 0:sp], psu[0:sp, 0:sp], AF.Copy, [psu], [Ub])
                R32 = m32.get()
                TTo(R32[0:sp, 0:sp], psu[0:sp, 0:sp], ident(sp), ALU.add, [psu, cst], [R32])
                Rb = mbf.get()
                ACT(Rb[0:sp, 0:sp], R32[0:sp, 0:sp], AF.Copy, [R32], [Rb])
                pw = 1
                while 2 * pw < C:
                    lastit = 4 * pw >= C
                    ps2 = psum.get()
                    MM(ps2[0:sp, 0:sp], Ub[0:sp, 0:sp], Vb[0:sp, 0:sp], [Ub, Vb], [ps2])
                    if not lastit:
                        MM(ps2[0:sp, 128:128 + sp], Vb[0:sp, 0:sp], Ub[0:sp, 0:sp], [Ub, Vb], [ps2])
                    Vb2 = mbf.get()
                    ACT(Vb2[0:sp, 0:sp], ps2[0:sp, 0:sp], AF.Copy, [ps2], [Vb2])
                    if not lastit:
                        Ub2 = mbf.get()
                        CP(Ub2[0:sp, 0:sp], ps2[0:sp, 128:128 + sp], [ps2], [Ub2])
                        Ub = Ub2
                    Vb = Vb2
                    ps3 = psum.get()
                    MM(ps3[0:sp, 0:sp], Vb[0:sp, 0:sp], Rb[0:sp, 0:sp], [Vb, Rb], [ps3])
                    TTo(R32[0:sp, 0:sp], R32[0:sp, 0:sp], ps3[0:sp, 0:sp], ALU.add, [R32, ps3], [R32])
                    Rb = mbf.get()
                    ACT(Rb[0:sp, 0:sp], R32[0:sp, 0:sp], AF.Copy, [R32], [Rb])
                    pw *= 2
                psw = psum.get()
                MM(psw[0:sp, 0:128], Rb[0:sp, 0:sp], B.vb[0:sp, sub, :], [Rb, B.vb], [psw])
                MM(psw[:, 128:128 + sp], B.kbg[0:sp, sub, :], Rb[0:sp, 0:sp], [Rb, B.kbg], [psw])
                u_sb = m32.get()
                ACT(u_sb[0:sp, :], psw[0:sp, 0:128], AF.Copy, [psw], [u_sb])
                wTb = gbf.get()
                CP(wTb[:, 0:sp], psw[:, 128:128 + sp], [psw], [wTb])
                for ci in range(sp // C):
                    co = ci * C
                    psv = psum.get()
                    MM(psv[0:sp, 0:128], wTb[:, 0:sp], Sb[:, h, :], [wTb, Sb.bs[h]], [psv])
                    vnew = gbf.get()
                    TTo(vnew[co:co + C, :], u_sb[co:co + C, :], psv[co:co + C, 0:128], ALU.subtract, [u_sb, psv], [vnew])
                    oc = slice(sub * 128 + co, sub * 128 + co + C)
                    MM(po[:, oc], Sb[:, h, :], qdT[:, co:co + C], [Sb.bs[h], qdT], [po], start=True, stop=False)
                    MM(po[:, oc], vnew[co:co + C, :], QKTb[co:co + C, co:co + C], [vnew, QKTb], [po], start=False, stop=True)
                    pss = psum.get()
                    MM(pss[:, 0:128], B.ktl[co:co + C, sub, :], vnew[co:co + C, :], [B.ktl, vnew], [pss])
                    STT(B.S[:, ls, h, :], B.S[:, ls, h, :], B.glb[:, sub, ci, h:h + 1], pss[:, 0:128], ALU.mult, ALU.add,
                        [Sbuf, B.glb, pss], [Sbuf])
                    ACT(Sb[:, h, :], B.S[:, ls, h, :], AF.Copy, [Sbuf], [Sb.bs[h]])
            if DBG is not None:
                CP(DBG[kind][:, h, 0:n], po[:, 0:n], [po], [DBG[kind]])
            sq = b1k.get()
            ACT(sq[:, 0:n], po[:, 0:n], AF.Square, [po], [sq])
            ps = psum.get()
            MM(ps[:, 0:n], ones_bf[:, :], sq[:, 0:n], [ones_bf, sq], [ps])
            r = B.stat[0]
            ACT(r[:, 0:n], ps[:, 0:n], AF.Sqrt, [ps], [r], bias=RMS_EPS, scale=1.0 / 128)
            RECIP(r[:, 0:n], r[:, 0:n], [r], [r])
            on = f2k.get()
            TTo(on[:, 0:n], po[:, 0:n], r[:, 0:n], ALU.mult, [po, r], [on])
            STT(B.gbT[:, h, 0:n], on[:, 0:n], smp[:, sb0 + 3:sb0 + 4], sgb[:, 0:n], ALU.mult, ALU.mult,
                [on, smp, sgb], [B.gbT.bs[h]])
        if emit_conv_out:
            DMA('sp', cv_o[kind][l], cvst[0:3, :], [cvst, OUTB], [], 'cvst_st')
            DMA('sp', st_o[kind][l].rearrange("h k v -> k h v"), B.S[:, ls, :, :], B.S.bs[ls * 8:(ls + 1) * 8] + [OUTB], [],
                kind + 'S_st')

        if STOP <= 4:
            return
        for j in range(16):
            cj = slice(j * 128, (j + 1) * 128)
            wma = wload('in', l, 64 + j)
            wmb = wload('in', l, 80 + j)
            ps_ma = proj_fm(B, wma, B.xTb)
            sa = f2k.get()
            ACT(sa[:, 0:n], ps_ma[:, 0:n], AF.Sigmoid, [ps_ma], [sa])
            ps_mb = proj_fm(B, wmb, B.xTb)
            sb_ = f2k.get()
            ACT(sb_[:, 0:n], ps_mb[:, 0:n], AF.Sigmoid, [ps_mb], [sb_])
            wfa = wload('of', l, j, kc=8)
            wfb = wload('og', l, j, kc=8)
            ps_ya = proj_fm(B, wfa, B.gaT, kc=8)
            TTo(sa[:, 0:n], sa[:, 0:n], ps_ya[:, 0:n], ALU.mult, [sa, ps_ya], [sa])
            ps_yb = proj_fm(B, wfb, B.gbT, kc=8)
            TTo(sb_[:, 0:n], sb_[:, 0:n], ps_yb[:, 0:n], ALU.mult, [sb_, ps_yb], [sb_])
            TTo(B.mT[:, j, 0:n], sa[:, 0:n], sb_[:, 0:n], ALU.add, [sa, sb_], [B.mT.bs[j]])
        if STOP <= 5:
            return
        for j in range(16):
            wj = wload('o', l, j)
            ps = proj_fm(B, wj, B.mT)
            STT(B.xT32[:, j, 0:n], B.xT32[:, j, 0:n], ALPHA, ps[:, 0:n], ALU.mult, ALU.add,
                [B.xT32.bs[j], ps], [B.xT32.bs[j]])
        pm, pq = ps_acc0, ps_acc1
        for j in range(16):
            ACT(B.xTb[:, j, 0:n], B.xT32[:, j, 0:n], AF.Copy, [B.xT32.bs[j]], [B.xTb.bs[j]])
            ACT(B.mT[:, j, 0:n], B.xT32[:, j, 0:n], AF.Square, [B.xT32.bs[j]], [B.mT.bs[j]])
            MM(pm[:, 0:n], ones_bf[:, :], B.xTb[:, j, 0:n], [ones_bf, B.xTb.bs[j]], [pm], start=(j == 0), stop=(j == 15))
            MM(pq[:, 0:n], ones_bf[:, :], B.mT[:, j, 0:n], [ones_bf, B.mT.bs[j]], [pq], start=(j == 0), stop=(j == 15))
        mean, rstd, tmpv = B.stat
        ACT(mean[:, 0:n], pm[:, 0:n], AF.Copy, [pm], [mean], scale=1.0 / D)
        TTo(tmpv[:, 0:n], mean[:, 0:n], mean[:, 0:n], ALU.mult, [mean], [tmpv])
        STT(tmpv[:, 0:n], pq[:, 0:n], 1.0 / D, tmpv[:, 0:n], ALU.mult, ALU.subtract, [pq, tmpv], [tmpv])
        ACT(rstd[:, 0:n], tmpv[:, 0:n], AF.Sqrt, [tmpv], [rstd], bias=LN_EPS)
        RECIP(rstd[:, 0:n], rstd[:, 0:n], [rstd], [rstd])
        for j in range(16):
            t_ = f2k.get()
            TTo(t_[:, 0:n], B.xT32[:, j, 0:n], mean[:, 0:n], ALU.subtract, [B.xT32.bs[j], mean], [t_])
            TTo(t_[:, 0:n], t_[:, 0:n], rstd[:, 0:n], ALU.mult, [t_, rstd], [t_])
            ACT(B.xT32[:, j, 0:n], t_[:, 0:n], AF.Identity, [t_, smp], [B.xT32.bs[j]],
                bias=smp[:, sb0 + 116 + j:sb0 + 117 + j], scale=smp[:, sb0 + 100 + j:sb0 + 101 + j])
            CP(B.xTb[:, j, 0:n], B.xT32[:, j, 0:n], [B.xT32.bs[j]], [B.xTb.bs[j]])
        if STOP <= 6:
            return
        DMA('sp', B.ptok[0:sp, :, :], p_d[kind][l, t0:t0 + n, :].rearrange("(s p) c -> p s c", p=sp), [], [B.ptok],
            kind + 'ptok')
        for sub in range(nsub):
            ps = psum.get()
            TR(ps[:, 0:sp], B.ptok[0:sp, sub, 0:128], ident(sp), [B.ptok, cst], [ps])
            TR(ps[:, 128:128 + sp], B.ptok[0:sp, sub, 128:256], ident(sp), [B.ptok, cst], [ps])
            CP(B.pTb[:, 0, sub * 128:sub * 128 + sp], ps[:, 0:sp], [ps], [B.pTb])
            CP(B.pTb[:, 1, sub * 128:sub * 128 + sp], ps[:, 128:128 + sp], [ps], [B.pTb])
        B.pTb.bs = [B.pTb.b, B.pTb.b]
        pe2 = ps_acc0
        for j in range(16):
            wpj = wload('pp', l, j, kc=2)
            ps = proj_fm(B, wpj, B.pTb, kc=2)
            CP(B.mT[:, j, 0:n], ps[:, 0:n], [ps], [B.mT.bs[j]])
            sq = b1k.get()
            ACT(sq[:, 0:n], ps[:, 0:n], AF.Square, [ps], [sq])
            MM(pe2[:, 0:n], ones_bf[:, :], sq[:, 0:n], [ones_bf, sq], [pe2], start=(j == 0), stop=(j == 15))
        rse = B.stat[0]
        ACT(rse[:, 0:n], pe2[:, 0:n], AF.Sqrt, [pe2], [rse], bias=RMS_EPS, scale=1.0 / D)
        RECIP(rse[:, 0:n], rse[:, 0:n], [rse], [rse])
        for j in range(16):
            wg = wload('pg', l, j)
            ps = proj_fm(B, wg, B.xTb)
            sg_ = f2k.get()
            ACT(sg_[:, 0:n], ps[:, 0:n], AF.Sigmoid, [ps], [sg_])
            t_ = f2k.get()
            TTo(t_[:, 0:n], B.mT[:, j, 0:n], rse[:, 0:n], ALU.mult, [B.mT.bs[j], rse], [t_])
            STT(t_[:, 0:n], t_[:, 0:n], smp[:, sb0 + 132 + j:sb0 + 133 + j], sg_[:, 0:n], ALU.mult, ALU.mult,
                [t_, smp, sg_], [t_])
            TTo(B.xT32[:, j, 0:n], B.xT32[:, j, 0:n], t_[:, 0:n], ALU.add, [B.xT32.bs[j], t_], [B.xT32.bs[j]])
        for j in range(16):
            ACT(B.xTb[:, j, 0:n], B.xT32[:, j, 0:n], AF.Copy, [B.xT32.bs[j]], [B.xTb.bs[j]])
        prepass(l + 1)

    if os.environ.get('MK_PONLY') is None:
      load_tile('s', 0)
      if STOP > 1:
        for l in range(DEPTH):
            layer('s', l, 0, True)
      store_tile('s', 0)
    for Ti in range(NTILES if os.environ.get('MK_SONLY') is None else 0):
        if os.environ.get('MK_NOLOAD') is None:
            load_tile('p', Ti * NT)
        if STOP > 1:
            for l in range(DEPTH):
                layer('p', l, Ti * NT, Ti == NTILES - 1)
        if os.environ.get('MK_NOSTORE') is None:
            store_tile('p', Ti * NT)
    P.op('sp', None, [], [OUTB])
    P.emit(nc, es)
    es.close()
    return nc, P


_CACHE = {}


def _get_prog(DEPTH, SEQ, PAST, NT, DS):
    key = (DEPTH, SEQ, PAST, NT, DS)
    if key not in _CACHE:
        _CACHE[key] = build(DEPTH, SEQ, PAST, NT, DS)
    return _CACHE[key]


def kernel(x_prompt, x_sample, cache_fox_k, cache_fox_v, cache_fox_logf, state_gdn_conv, state_gdn,
           p_prompt, p_sample, w_in, fox_f_bias, gdn_conv_w, gdn_a_log, gdn_dt_bias, gdn_norm_w,
           w_out_fox, w_out_gdn, w_out, ln_g, ln_b, w_pl_proj, w_pl_gate, pl_norm_w, NT=None, n_cores=None):
    f = lambda a: np.ascontiguousarray(np.asarray(a, dtype=np.float32))
    x_prompt, x_sample = f(x_prompt), f(x_sample)
    BATCH, SEQ, _ = x_prompt.shape
    DS = x_sample.shape[1]
    DEPTH = w_in.shape[0]
    PAST = cache_fox_k.shape[2]
    if NT is None:
        NT = 256
    nco = BATCH if n_cores is None else n_cores
    nc, _P = _get_prog(DEPTH, SEQ, PAST, NT, DS)
    inp = dict(fox_f_bias=f(fox_f_bias), gdn_dt_bias=f(gdn_dt_bias), gdn_a_log=f(gdn_a_log), gdn_norm_w=f(gdn_norm_w),
               gdn_conv_w=f(gdn_conv_w), ln_g=f(ln_g), ln_b=f(ln_b), pl_norm_w=f(pl_norm_w))
    def tile_w(w, kc):
        L_, K_, N_ = w.shape
        return np.ascontiguousarray(w.reshape(L_, kc, 128, N_ // 128, 128).transpose(0, 3, 2, 1, 4)).reshape(
            L_, N_ // 128, 128, kc * 128)
    w_in = f(w_in)
    w_main = np.concatenate([w_in[:, :, 0:OFF_F], w_in[:, :, OFF_GA:OFF_A], w_in[:, :, OFF_GB:NIN]], axis=2)
    wsm_h = np.zeros((DEPTH, D, 96), np.float32)
    wsm_h[:, :, 0:8] = w_in[:, :, OFF_F:OFF_F + 8]
    wsm_h[:, :, 32:40] = w_in[:, :, OFF_A:OFF_A + 8]
    wsm_h[:, :, 64:72] = w_in[:, :, OFF_BETA:OFF_BETA + 8]
    wsm_h = np.ascontiguousarray(wsm_h.reshape(DEPTH, 16, 128, 96).transpose(0, 2, 1, 3)).reshape(DEPTH, 128, 16 * 96)
    shared = dict(wt_in=tile_w(w_main, 16), wt_of=tile_w(f(w_out_fox), 8), wt_og=tile_w(f(w_out_gdn), 8),
                  wt_o=tile_w(f(w_out), 16), wt_pp=tile_w(f(w_pl_proj), 2), wt_pg=tile_w(f(w_pl_gate), 16),
                  wsm_d=wsm_h, smallp=make_smallp(inp, DEPTH), consts=make_consts())
    del w_main
    ck, cv, clf = f(cache_fox_k), f(cache_fox_v), f(cache_fox_logf)
    scv, sg, pp, ps = f(state_gdn_conv), f(state_gdn), f(p_prompt), f(p_sample)
    in_maps = []
    for b in range(nco):
        m = dict(shared)
        m.update(xp=x_prompt[b], xs=x_sample[b],
                 ck=np.ascontiguousarray(ck[:, b].reshape(DEPTH, PAST, 1024)),
                 cv=np.ascontiguousarray(cv[:, b].reshape(DEPTH, PAST, 1024)),
                 clf=np.ascontiguousarray(clf[:, b]), scv=np.ascontiguousarray(scv[:, b]),
                 sg=np.ascontiguousarray(sg[:, b]), pp=np.ascontiguousarray(pp[:, b]),
                 ps=np.ascontiguousarray(ps[:, b]))
        in_maps.append(m)
    res = run_bass_kernel_spmd(nc, in_maps, core_ids=list(range(nco)))
    R = res.results
    st = lambda name, ax: np.stack([np.asarray(R[b][name]) for b in range(nco)], axis=ax)
    yp = st('yp', 0)
    ys = st('ys', 0)
    kp = st('kp', 1).reshape(DEPTH, nco, SEQ, 8, 128)
    vp = st('vp', 1).reshape(DEPTH, nco, SEQ, 8, 128)
    lfp = st('lfp', 1)
    cvp = st('cvp', 1)
    stp = st('stp', 1)
    ks = st('ks', 1).reshape(DEPTH, nco, DS, 8, 128)
    vs = st('vs', 1).reshape(DEPTH, nco, DS, 8, 128)
    lfs = st('lfs', 1)
    cvs = st('cvs', 1)
    sts = st('sts', 1)
    return (yp, ys, kp, vp, lfp, cvp, stp, ks, vs, lfs, cvs, sts)
```

```python
import bisect
import os
import numpy as np
import concourse.bass as bass
import concourse.mybir as mybir
from concourse.bass_utils import run_bass_kernel_spmd
from contextlib import ExitStack

F32 = mybir.dt.float32
BF16 = mybir.dt.bfloat16
AF = mybir.ActivationFunctionType
ALU = mybir.AluOpType

ENGS = ('pe', 'act', 'dve', 'pool', 'sp')


class Buf:
    __slots__ = ('name', 'lw', 'rd', 'x')

    def __init__(self, name='', x=False):
        self.name = name
        self.lw = None
        self.rd = []
        self.x = x


class Op:
    __slots__ = ('eng', 'idx', 'fn', 'deps', 'inc', 'semkey', 'cum', 'isdma', 'waits', 'semval')


class Prog:
    def __init__(self):
        self.ops = {e: [] for e in ENGS}
        self.dma_cum = {}
        self.order = []

    def op(self, eng, fn, reads=(), writes=(), dma_sem=None):
        o = Op()
        o.eng = eng; o.fn = fn; o.inc = False; o.isdma = dma_sem is not None
        o.semkey = dma_sem; o.cum = 0; o.waits = None; o.semval = 0
        xr = [b for b in reads if b.x]
        if xr:
            reads = [b for b in reads if not b.x]
            writes = list(writes) + xr
        lst = self.ops[eng]
        o.idx = len(lst)
        lst.append(o)
        self.order.append(o)
        deps = []
        for b in reads:
            if b.lw is not None:
                deps.append(b.lw)
        for b in writes:
            if b.lw is not None:
                deps.append(b.lw)
            deps.extend(b.rd)
        if eng == 'pe' and not o.isdma:
            deps = [d for d in deps if d.isdma or d.eng != 'pe']
        o.deps = deps
        if o.isdma:
            c = self.dma_cum.get(dma_sem, 0) + 16
            self.dma_cum[dma_sem] = c
            o.cum = c
        for b in reads:
            b.rd.append(o)
        for b in writes:
            b.lw = o
            b.rd = []
        return o

    def plan(self):
        known = {e: {} for e in ENGS}
        hidx = {e: [] for e in ENGS}
        hsnap = {e: [] for e in ENGS}
        nw = 0
        for o in self.order:
            F = o.eng
            kn = known[F]
            waits = []
            ds = sorted(set(o.deps), key=lambda d: -(d.cum if d.isdma else d.idx))
            for d in ds:
                if d is o:
                    continue
                if d.isdma:
                    key = ('d', d.semkey); val = d.cum
                else:
                    key = ('c', d.eng); val = d.idx
                if kn.get(key, -1) >= val:
                    continue
                waits.append(d)
                if not d.isdma:
                    d.inc = True
                kn[key] = val
                E = d.eng
                hi = hidx[E]
                p = bisect.bisect_right(hi, d.idx) - 1
                if p >= 0:
                    for k2, v2 in hsnap[E][p].items():
                        if kn.get(k2, -1) < v2:
                            kn[k2] = v2
            o.waits = waits
            if waits:
                nw += len(waits)
                hidx[F].append(o.idx)
                hsnap[F].append(dict(kn))
        for e in ENGS:
            c = 0
            for o in self.ops[e]:
                if o.inc:
                    c += 1
                o.semval = c
        self.nwaits = nw

    def emit(self, nc, es):
        self.plan()
        csem = {e: es.enter_context(nc.semaphore('c_' + e)) for e in ENGS}
        dsem = {k: es.enter_context(nc.semaphore('d_%d' % i)) for i, k in enumerate(self.dma_cum)}
        block = es.enter_context(nc.Block())
        ops = self.ops

        def run(e, eng):
            mysem = csem[e]
            for o in ops[e]:
                for d in o.waits:
                    if d.isdma:
                        eng.wait_ge(dsem[d.semkey], d.cum)
                    else:
                        eng.wait_ge(csem[d.eng], d.semval)
                if o.fn is None:
                    continue
                ins = o.fn(eng)
                if o.isdma:
                    ins.then_inc(dsem[o.semkey], 16)
                elif o.inc:
                    ins.then_inc(mysem, 1)

        @block.sync
        def _(eng):
            run('sp', eng)

        @block.tensor
        def _(eng):
            run('pe', eng)

        @block.scalar
        def _(eng):
            run('act', eng)

        @block.vector
        def _(eng):
            run('dve', eng)

        @block.gpsimd
        def _(eng):
            run('pool', eng)


D = 2048
KC = 16
H = 8
NIN = 12312
OFF_Q, OFF_K, OFF_V, OFF_F, OFF_GA = 0, 1024, 2048, 3072, 3080
OFF_QKVB, OFF_A, OFF_BETA, OFF_GB, OFF_MA, OFF_MB = 4104, 7176, 7184, 7192, 8216, 10264
SPL = 148
LN_EPS = 1e-5
RMS_EPS = 1e-6
NEG = -30000.0

C_ID, C_ONES, C_BU, C_SAMEC, C_CMA, C_CMB, C_POSLS, C_NEGU, C_NEGA, C_SEL, C_SUT = (
    0, 128, 256, 384, 512, 640, 768, 896, 1024, 1152, 2176)
NCON = 2304


def make_consts():
    c = np.zeros((128, NCON), np.float32)
    i = np.arange(128)
    same = (i[:, None] // 64) == (i[None, :] // 64)
    c[:, C_ID:C_ID + 128] = np.eye(128)
    c[:, C_ONES:C_ONES + 128] = 1.0
    c[:, C_BU:C_BU + 128] = ((i[:, None] <= i[None, :]) & same)
    c[:, C_SAMEC:C_SAMEC + 128] = same
    c[:, C_CMA:C_CMA + 128] = (i[:, None] < 64)
    c[:, C_CMB:C_CMB + 128] = (i[:, None] >= 64)
    c[:, C_POSLS:C_POSLS + 128] = np.where((i[:, None] > i[None, :]) & same, 0.0, -NEG)
    c[:, C_NEGU:C_NEGU + 128] = np.where((i[:, None] <= i[None, :]) & same, 0.0, NEG)
    c[:, C_NEGA:C_NEGA + 128] = np.where(i[:, None] <= i[None, :], 0.0, NEG)
    for h in range(8):
        c[h, C_SEL + h * 128:C_SEL + (h + 1) * 128] = 1.0
    c[:, C_SUT:C_SUT + 128] = (i[:, None] < i[None, :])
    return c


def make_smallp(inp, depth):
    sp = np.zeros((128, depth * SPL), np.float32)
    for l in range(depth):
        b = l * SPL
        sp[0:8, b + 0] = inp['fox_f_bias'][l]
        sp[32:40, b + 1] = inp['gdn_dt_bias'][l]
        sp[32:40, b + 2] = inp['gdn_a_log'][l]
        sp[:, b + 3] = inp['gdn_norm_w'][l]
        cw = inp['gdn_conv_w'][l]
        sp[:, b + 4:b + 100] = cw.reshape(4, 24, 128).transpose(2, 1, 0).reshape(128, 96)
        sp[:, b + 100:b + 116] = inp['ln_g'][l].reshape(16, 128).T
        sp[:, b + 116:b + 132] = inp['ln_b'][l].reshape(16, 128).T
        sp[:, b + 132:b + 148] = inp['pl_norm_w'][l].reshape(16, 128).T
    return sp


def build(DEPTH, SEQ, PAST, NT, DS=16, ALPHA=(2 * 4) ** 0.25):
    nc = bass.Bass('TRN2', target_bir_lowering=False)
    P = Prog()
    es = ExitStack()
    FOX_SCALE = 128 ** -0.5
    GDN_SCALE = 128 ** -0.5
    NTILES = SEQ // NT
    PS_ = PAST // 128

    def din(n, s, dt=F32):
        return nc.dram_tensor(n, list(s), dt, kind='ExternalInput').ap()

    def dout(n, s):
        return nc.dram_tensor(n, list(s), F32, kind='ExternalOutput').ap()

    def dscr(n, s, dt):
        return nc.dram_tensor(n, list(s), dt, kind='Internal').ap()

    x_d = {'p': din('xp', [SEQ, D]), 's': din('xs', [DS, D])}
    ck_d = din('ck', [DEPTH, PAST, 1024]); cv_d = din('cv', [DEPTH, PAST, 1024])
    clf_d = din('clf', [DEPTH, PAST, 8])
    scv_d = din('scv', [DEPTH, 3, 3072]); sg_d = din('sg', [DEPTH, 8, 128, 128])
    p_d = {'p': din('pp', [DEPTH, SEQ, 256]), 's': din('ps', [DEPTH, DS, 256])}
    WSPEC = (('in', 96, 2048), ('of', 16, 1024), ('og', 16, 1024), ('o', 16, 2048), ('pp', 16, 256), ('pg', 16, 2048))
    wt_d = {k: din('wt_' + k, [DEPTH, nt_, 128, wd_]) for k, nt_, wd_ in WSPEC}
    ws_d = {k: dscr('ws_' + k, [DEPTH, nt_, 128, wd_], BF16) for k, nt_, wd_ in WSPEC}
    wsm_d = din('wsm_d', [DEPTH, 128, 16 * 96])
    wscr = [Buf('wscr%d' % l) for l in range(DEPTH)]
    smallp_d = din('smallp', [128, DEPTH * SPL]); consts_d = din('consts', [128, NCON])
    y_d = {'p': dout('yp', [SEQ, D]), 's': dout('ys', [DS, D])}
    k_o = {'p': dout('kp', [DEPTH, SEQ, 1024]), 's': dout('ks', [DEPTH, DS, 1024])}
    v_o = {'p': dout('vp', [DEPTH, SEQ, 1024]), 's': dout('vs', [DEPTH, DS, 1024])}
    lf_o = {'p': dout('lfp', [DEPTH, SEQ, 8]), 's': dout('lfs', [DEPTH, DS, 8])}
    cv_o = {'p': dout('cvp', [DEPTH, 3, 3072]), 's': dout('cvs', [DEPTH, 3, 3072])}
    st_o = {'p': dout('stp', [DEPTH, 8, 128, 128]), 's': dout('sts', [DEPTH, 8, 128, 128])}
    kT_scr = dscr('kTscr', [DEPTH, 8, 128, SEQ], BF16)
    v_scr = dscr('vscr', [DEPTH, 8, 128, SEQ // 128, 128], BF16)
    OUTB = Buf('outputs')

    class T:
        def __init__(s, name, shape, dt, nb=0):
            s.h = es.enter_context(nc.sbuf_tensor(name, list(shape), dt))
            s.b = Buf(name)
            s.bs = [Buf('%s_%d' % (name, i)) for i in range(nb)]
            s.name = name

        def __getitem__(s, k):
            return s.h[k]

    class RPool:
        def __init__(s, name, shape, dt, n):
            s.ts = [T('%s%d' % (name, i), shape, dt) for i in range(n)]
            s.i = 0

        def get(s):
            t = s.ts[s.i % len(s.ts)]
            s.i += 1
            return t

    class PS:
        def __init__(s, name):
            s.h = es.enter_context(nc.psum_tensor(name, [128, 512], F32))
            s.b = Buf(name, x=True)

        def __getitem__(s, k):
            return s.h[k]

    psall = [PS('ps%d' % i) for i in range(8)]
    ps_acc0, ps_acc1 = psall[0], psall[1]

    class PSPool:
        def __init__(s):
            s.i = 0

        def get(s):
            t = psall[2 + s.i % 6]
            s.i += 1
            return t
    psum = PSPool()

    def bl(xs):
        out = []
        for x in xs:
            if isinstance(x, Buf):
                out.append(x)
            elif isinstance(x, (list, tuple)):
                out.extend(bl(x))
            else:
                out.append(x.b)
        return out

    def MM(out, lhsT, rhs, R, W, start=True, stop=True):
        P.op('pe', lambda e: e.matmul(out, lhsT, rhs, start=start, stop=stop), bl(R), bl(W))

    def TR(out, in_, ident, R, W):
        P.op('pe', lambda e: e.transpose(out, in_, ident), bl(R), bl(W))

    def ACT(out, in_, func, R, W, bias=None, scale=None):
        kw = {}
        if bias is not None:
            kw['bias'] = bias
        if scale is not None:
            kw['scale'] = scale
        P.op('act', lambda e: e.activation(out=out, in_=in_, func=func, **kw), bl(R), bl(W))

    def TTo(out, in0, in1, op, R, W, eng='dve'):
        P.op(eng, lambda e: e.tensor_tensor(out=out, in0=in0, in1=in1, op=op), bl(R), bl(W))

    def TS(out, in0, s1, op0, R, W, s2=None, op1=None, eng='dve'):
        if op1 is None:
            P.op(eng, lambda e: e.tensor_scalar(out=out, in0=in0, scalar1=s1, scalar2=None, op0=op0), bl(R), bl(W))
        else:
            P.op(eng, lambda e: e.tensor_scalar(out=out, in0=in0, scalar1=s1, scalar2=s2, op0=op0, op1=op1),
                 bl(R), bl(W))

    def STT(out, in0, scalar, in1, op0, op1, R, W):
        P.op('dve', lambda e: e.scalar_tensor_tensor(out=out, in0=in0, scalar=scalar, in1=in1, op0=op0, op1=op1),
             bl(R), bl(W))

    def CP(out, in_, R, W, eng='dve'):
        if eng == 'act':
            P.op('act', lambda e: e.copy(out=out, in_=in_), bl(R), bl(W))
        else:
            P.op(eng, lambda e: e.tensor_scalar(out=out, in0=in_, scalar1=1.0, scalar2=None, op0=ALU.mult),
                 bl(R), bl(W))

    def RECIP(out, in_, R, W):
        P.op('dve', lambda e: e.reciprocal(out=out, in_=in_), bl(R), bl(W))

    def MEMSET(t_ap, val, W, eng='dve'):
        P.op(eng, lambda e: e.memset(t_ap, val), [], bl(W))

    def DMA(q, out, in_, R, W, sem):
        P.op(q, lambda e: e.dma_start(out=out, in_=in_), bl(R), bl(W), dma_sem=sem + ('_sw' if q == 'pool' else ''))

    cst = T('cst', [128, NCON], F32)
    smp = T('smp', [128, DEPTH * SPL], F32)
    nsm = T('nsm', [128, DEPTH * SPL], F32)
    nexpA = T('nexpA', [128, DEPTH], F32)
    ones_bf = T('ones_bf', [128, 128], BF16)
    onesN = T('onesN', [128, max(NT, 128)], F32)
    negA_rep = T('negA_rep', [128, max(NT, 128)], F32)
    DMA('sp', cst[:, :], consts_d, [], [cst], 'cst')
    DMA('sp', smp[:, :], smallp_d, [], [smp], 'smp')
    CP(ones_bf[:, :], cst[:, C_ONES:C_ONES + 128], [cst], [ones_bf])
    MEMSET(onesN[:, :], 1.0, [onesN])
    for i in range(max(NT, 128) // 128):
        CP(negA_rep[:, i * 128:(i + 1) * 128], cst[:, C_NEGA:C_NEGA + 128], [cst], [negA_rep])
    TS(nsm[:, :], smp[:, :], -1.0, ALU.mult, [smp], [nsm])
    for l in range(DEPTH):
        ACT(nexpA[:, l:l + 1], smp[:, l * SPL + 2:l * SPL + 3], AF.Exp, [smp], [nexpA])
    TS(nexpA[:, :], nexpA[:, :], -1.0, ALU.mult, [nexpA], [nexpA])
    ident = lambda a, b_=None: cst[0:a, C_ID:C_ID + (a if b_ is None else b_)]

    wpool = RPool('w', [128, 16, 128], BF16, 4)
    wsm = T('wsm', [128, 16, 96], BF16)
    f2k = RPool('f2k', [128, NT], F32, 4)
    b1k = RPool('b1k', [128, NT], BF16, 3)
    tpool = RPool('tp', [128, NT], F32, 3)
    ppool = RPool('pp', [128, NT], BF16, 4)
    xppool = RPool('xpp', [128, NT + 3], F32, 2)
    m32 = RPool('m32', [128, 128], F32, 6)
    mbf = RPool('mbf', [128, 128], BF16, 8)
    gbf = RPool('gbf', [128, 128], BF16, 6)
    kst = RPool('kst', [128, NT // 128, 128], F32, 2)
    vst = RPool('vst', [128, NT // 128, 128], F32, 2)
    NKT = max(SEQ, PAST + 128)
    KTp = RPool('KT', [128, NKT], BF16, 2)
    VTp = RPool('VT', [128, NKT // 128, 128], BF16, 2)
    ktok32 = T('ktok32', [128, max(PS_, 24), 128], F32)
    class View:
        def __init__(s, base_ap, b, name):
            s.ap = base_ap
            s.b = b
            s.name = name

        def __getitem__(s, k):
            return s.ap[k]
    cvst = View(ktok32.h[0:3, 0:24, :].rearrange("p a b -> p (a b)"), ktok32.b, 'cvst')
    xtok = View(ktok32.h[:, 0:16, :].rearrange("p a b -> p (a b)"), ktok32.b, 'xtok')
    Sb = T('Sb', [128, 8, 128], BF16, nb=8)
    kscrB = [[Buf('kscr%d_%d' % (l, h)) for h in range(8)] for l in range(DEPTH)]

    def wload(key, l, ti, kc=16):
        t = wpool.get()
        DMA('pool', t[:, 0:kc, :], ws_d[key][l, ti].rearrange("p (k c) -> p k c", k=kc), [wscr[l]], [t], t.name)
        return t

    pre_done = set()

    def prepass(l):
        if l >= DEPTH or l in pre_done:
            return
        pre_done.add(l)
        for key, nt_, wd_ in WSPEC:
            tot = nt_ * wd_
            src = wt_d[key][l].rearrange("t p c -> (t p) c").rearrange("(a b) c -> a (b c)", a=128)
            dst = ws_d[key][l].rearrange("t p c -> (t p) c").rearrange("(a b) c -> a (b c)", a=128)
            step = 8192
            for c0 in range(0, tot, step):
                c1 = min(tot, c0 + step)
                DMA('pool', dst[:, c0:c1], src[:, c0:c1], [], [wscr[l]], 'pre%d' % l)

    class KB:
        pass
    KBs = {}
    for kind, n in (('p', NT), ('s', DS)):
        B = KB()
        B.n = n
        B.sp = min(n, 128)
        B.nsub = max(1, n // 128)
        B.C = 64 if kind == 'p' else 16
        B.xT32 = T(kind + 'xT32', [128, 16, n], F32, nb=16)
        B.xTb = T(kind + 'xTb', [128, 16, n], BF16, nb=16)
        B.gaT = T(kind + 'gaT', [128, 8, n], BF16, nb=8)
        B.gbT = T(kind + 'gbT', [128, 8, n], BF16, nb=8)
        B.mT = T(kind + 'mT', [128, 16, n], BF16, nb=16)
        B.smT = T(kind + 'smT', [128, n], F32)
        B.scT = T(kind + 'scT', [128, n], F32)
        B.smtok = T(kind + 'smtok', [128, B.nsub, 128], F32)
        B.sctok = T(kind + 'sctok', [128, B.nsub, 128], F32)
        B.carry = T(kind + 'carry', [128, DEPTH], F32)
        nsa = (SEQ // 128) if kind == 'p' else 1
        B.ncall = T(kind + 'ncall', [128, DEPTH, nsa, 8], F32)
        B.gctok = T(kind + 'gctok', [128, B.nsub, 8], F32)
        B.ngctok = T(kind + 'ngctok', [128, B.nsub, 8], F32)
        B.gcT = T(kind + 'gcT', [8, n], F32)
        B.bege = T(kind + 'bege', [128, B.nsub, 8], F32)
        B.nbeta = T(kind + 'nbeta', [128, B.nsub, 8], F32)
        B.tail = T(kind + 'tail', [128, B.nsub, 8], F32)
        B.glb = T(kind + 'glb', [128, B.nsub, 2, 8], F32)
        B.SD = DEPTH if kind == 'p' else 1
        B.S = T(kind + 'S', [128, B.SD, 8, 128], F32, nb=B.SD * 8)
        B.convc = T(kind + 'convc', [128, DEPTH, 24, 3], F32)
        B.cqb = T(kind + 'cqb', [128, n], F32)
        B.cqm = T(kind + 'cqm', [128, n], F32)
        B.qT = T(kind + 'qT', [128, n], BF16)
        B.sga = T(kind + 'sga', [128, n], F32)
        B.kT32 = T(kind + 'kT32', [128, n], F32)
        B.vT32 = T(kind + 'vT32', [128, n], F32)
        B.qTn = T(kind + 'qTn', [128, n], BF16)
        B.kTn = T(kind + 'kTn', [128, n], BF16)
        B.kbg = T(kind + 'kbg', [128, B.nsub, 128], BF16)
        B.ktl = T(kind + 'ktl', [128, B.nsub, 128], BF16)
        B.vb = T(kind + 'vb', [128, B.nsub, 128], BF16)
        B.ptok = T(kind + 'ptok', [128, B.nsub, 256], F32)
        B.pTb = T(kind + 'pTb', [128, 2, n], BF16)
        B.stat = [T(kind + 'stat%d' % i, [128, n], F32) for i in range(3)]
        MEMSET(B.smT[:, :], 0.0, [B.smT])
        MEMSET(B.scT[:, :], 0.0, [B.scT])
        MEMSET(B.carry[:, :], 0.0, [B.carry])
        KBs[kind] = B
    DBG = None
    if os.environ.get('MK_DEBUG'):
        DBG = {'p': T('dbg_p', [128, 8, NT], F32), 's': T('dbg_s', [128, 8, DS], F32)}
    lfp = T('lfpast', [128, PS_, 8], F32)
    cpast = T('cpast', [128, 8, PS_], F32)
    ncpast = T('ncpast', [128, 8, PS_], F32)
    pre = T('pre', [128, 8], F32)
    Bp = KBs['p']
    MEMSET(Bp.S[:, :, :, :], 0.0, Bp.S.bs)
    MEMSET(Bp.convc[:, :, :, :], 0.0, [Bp.convc])

    def proj_fm(B, wt, src, kc=16, mcols=128):
        n = B.n
        ps = psum.get()
        for k in range(kc):
            MM(ps[0:mcols, 0:n], wt[:, k, 0:mcols], src[:, k, 0:n], [wt, src.bs[k]], [ps],
               start=(k == 0), stop=(k == kc - 1))
        return ps

    def load_tile(kind, t0):
        B = KBs[kind]
        n, sp, nsub = B.n, B.sp, B.nsub
        for sub in range(nsub):
            DMA('sp', xtok[0:sp, :], x_d[kind][t0 + sub * 128:t0 + sub * 128 + sp, :], [], [xtok], 'xtok')
            for kg in range(4):
                ps = psum.get()
                for j in range(4):
                    k = kg * 4 + j
                    TR(ps[:, j * 128:j * 128 + sp], xtok[0:sp, k * 128:(k + 1) * 128], ident(sp), [xtok, cst], [ps])
                src = ps[:, :].rearrange("p (j t) -> p j t", j=4)[:, :, 0:sp]
                ACT(B.xT32[:, kg * 4:(kg + 1) * 4, sub * 128:sub * 128 + sp], src, AF.Copy, [ps],
                    B.xT32.bs[kg * 4:(kg + 1) * 4])
                if os.environ.get('MK_NODVE') is None:
                    CP(B.xTb[:, kg * 4:(kg + 1) * 4, sub * 128:sub * 128 + sp], src, [ps], B.xTb.bs[kg * 4:(kg + 1) * 4])

    def store_tile(kind, t0):
        B = KBs[kind]
        n, sp, nsub = B.n, B.sp, B.nsub
        for sub in range(nsub):
            for kg in range(4):
                ps = psum.get()
                for j in range(4):
                    k = kg * 4 + j
                    TR(ps[0:sp, j * 128:(j + 1) * 128], B.xT32[:, k, sub * 128:sub * 128 + sp], ident(128),
                       [B.xT32.bs[k], cst], [ps])
                if kg % 2 == 0:
                    ACT(xtok[0:sp, kg * 512:(kg + 1) * 512], ps[0:sp, :], AF.Copy, [ps], [xtok])
                else:
                    CP(xtok[0:sp, kg * 512:(kg + 1) * 512], ps[0:sp, :], [ps], [xtok])
            DMA('sp', y_d[kind][t0 + sub * 128:t0 + sub * 128 + sp, :], xtok[0:sp, :], [xtok, OUTB], [], 'xtok_st')

    STOP = float(os.environ.get('MK_STOP', '99'))

    def layer(kind, l, t0, last_tile):
        B = KBs[kind]
        n, sp, nsub, C = B.n, B.sp, B.nsub, B.C
        sb0 = l * SPL
        ls = l if kind == 'p' else 0
        gsub0 = t0 // 128 if kind == 'p' else 0
        prepass(l)

        DMA('pool', wsm[:, :, :], wsm_d[l].rearrange("p (k c) -> p k c", k=16), [], [wsm], 'wsm')
        if STOP <= 1.2:
            return
        if kind == 's':
            DMA('sp', lfp[:, :, :], clf_d[l].rearrange("(p s) h -> p s h", s=PS_), [], [lfp], 'lfp')
            for h in range(8):
                P.op('dve', lambda e, h=h: e.tensor_tensor_scan(out=cpast[:, h, :], data0=onesN[:, 0:PS_],
                                                                 data1=lfp[:, :, h], initial=0.0,
                                                                 op0=ALU.mult, op1=ALU.add), bl([lfp, onesN]), bl([cpast]))
            ps = psum.get()
            MM(ps[:, 0:8], cst[:, C_SUT:C_SUT + 128], cpast[:, :, PS_ - 1], [cst, cpast], [ps])
            MM(ps[0:8, 16:17], cpast[:, :, PS_ - 1], cst[:, C_ONES:C_ONES + 1], [cst, cpast], [ps])
            CP(pre[:, :], ps[:, 0:8], [ps], [pre])
            CP(B.carry[0:8, l:l + 1], ps[0:8, 16:17], [ps], [B.carry])
            for h in range(8):
                TS(cpast[:, h, :], cpast[:, h, :], pre[:, h:h + 1], ALU.add, [cpast, pre], [cpast])
            TS(ncpast[:, :, :], cpast[:, :, :], -1.0, ALU.mult, [cpast], [ncpast])
            DMA('sp', B.S[:, ls, :, :], sg_d[l].rearrange("h k v -> k h v"), [], B.S.bs[ls * 8:(ls + 1) * 8], 'sgl')
            DMA('sp', cvst[:, :], scv_d[l], [], [cvst], 'cvst_l')
            for cid in range(24):
                ps = psum.get()
                TR(ps[:, 0:3], cvst[0:3, cid * 128:(cid + 1) * 128], ident(3), [cvst, cst], [ps])
                CP(B.convc[:, l, cid, :], ps[:, 0:3], [ps], [B.convc])

        if STOP <= 1.4:
            return
        ps = proj_fm(B, wsm, B.xTb, 16, 96)
        ACT(B.smT[0:8, 0:n], ps[0:8, 0:n], AF.Exp, [ps, nsm], [B.smT], bias=nsm[0:8, sb0:sb0 + 1], scale=-1.0)
        ACT(B.smT[0:8, 0:n], B.smT[0:8, 0:n], AF.Ln, [B.smT], [B.smT], bias=1.0)
        TS(B.smT[0:8, 0:n], B.smT[0:8, 0:n], -1.0, ALU.mult, [B.smT], [B.smT])
        ACT(B.smT[32:40, 0:n], ps[32:40, 0:n], AF.Exp, [ps, smp], [B.smT], bias=smp[32:40, sb0 + 1:sb0 + 2])
        ACT(B.smT[32:40, 0:n], B.smT[32:40, 0:n], AF.Ln, [B.smT], [B.smT], bias=1.0)
        TS(B.smT[32:40, 0:n], B.smT[32:40, 0:n], nexpA[32:40, l:l + 1], ALU.mult, [B.smT, nexpA], [B.smT])
        ACT(B.smT[64:72, 0:n], ps[64:72, 0:n], AF.Sigmoid, [ps], [B.smT])
        if STOP <= 1.6:
            return
        P.op('dve', lambda e: e.tensor_tensor_scan(out=B.scT[:, 0:n], data0=onesN[:, 0:n], data1=B.smT[:, 0:n],
                                                   initial=B.carry[:, l:l + 1], op0=ALU.mult, op1=ALU.add),
             bl([B.smT, onesN, B.carry]), bl([B.scT]))
        CP(B.carry[:, l:l + 1], B.scT[:, n - 1:n], [B.scT], [B.carry])
        if STOP <= 1.65:
            return
        for sub in range(nsub):
            cs = slice(sub * 128, sub * 128 + sp)
            ps = psum.get()
            TR(ps[0:sp, 0:128], B.smT[:, cs], ident(128), [B.smT, cst], [ps])
            if STOP > 1.7:
                TR(ps[0:sp, 128:256], B.scT[:, cs], ident(128), [B.scT, cst], [ps])
            if STOP > 1.72:
                ACT(B.smtok[0:sp, sub, :], ps[0:sp, 0:128], AF.Copy, [ps], [B.smtok])
            if STOP > 1.74:
                ACT(B.sctok[0:sp, sub, :], ps[0:sp, 128:256], AF.Copy, [ps], [B.sctok])
        if STOP <= 1.8:
            return
        TS(B.ncall[0:sp, l, gsub0:gsub0 + nsub, :], B.sctok[0:sp, :, 0:8], -1.0, ALU.mult, [B.sctok], [B.ncall])
        DMA('sp', lf_o[kind][l, t0:t0 + n, :].rearrange("(s p) h -> p s h", p=sp), B.smtok[0:sp, :, 0:8],
            [B.smtok, OUTB], [], kind + 'lf_st')
        for sub in range(nsub):
            g_tok = B.smtok[0:sp, sub, 32:40]
            ps = psum.get()
            MM(ps[0:sp, 0:8], cst[0:sp, C_BU:C_BU + sp], g_tok, [cst, B.smtok], [ps])
            MM(ps[0:8, 128:128 + sp], g_tok, cst[0:sp, C_BU:C_BU + sp], [cst, B.smtok], [ps])
            MM(ps[0:sp, 256:264], cst[0:sp, C_SAMEC:C_SAMEC + sp], g_tok, [cst, B.smtok], [ps])
            MM(ps[:, 272:280], cst[0:sp, C_CMA:C_CMA + 128], g_tok, [cst, B.smtok], [ps])
            MM(ps[:, 288:296], cst[0:sp, C_CMB:C_CMB + 128], g_tok, [cst, B.smtok], [ps])
            CP(B.gctok[0:sp, sub, :], ps[0:sp, 0:8], [ps], [B.gctok])
            TS(B.ngctok[0:sp, sub, :], ps[0:sp, 0:8], -1.0, ALU.mult, [ps], [B.ngctok])
            CP(B.gcT[0:8, sub * 128:sub * 128 + sp], ps[0:8, 128:128 + sp], [ps], [B.gcT])
            ACT(B.bege[0:sp, sub, :], ps[0:sp, 0:8], AF.Exp, [ps], [B.bege])
            TTo(B.tail[0:sp, sub, :], ps[0:sp, 256:264], B.gctok[0:sp, sub, :], ALU.subtract, [ps, B.gctok], [B.tail])
            ACT(B.tail[0:sp, sub, :], B.tail[0:sp, sub, :], AF.Exp, [B.tail], [B.tail])
            ACT(B.glb[:, sub, 0, :], ps[:, 272:280], AF.Exp, [ps], [B.glb])
            ACT(B.glb[:, sub, 1, :], ps[:, 288:296], AF.Exp, [ps], [B.glb])
            TTo(B.bege[0:sp, sub, :], B.bege[0:sp, sub, :], B.smtok[0:sp, sub, 64:72], ALU.mult,
                [B.bege, B.smtok], [B.bege])
            TS(B.nbeta[0:sp, sub, :], B.smtok[0:sp, sub, 64:72], -1.0, ALU.mult, [B.smtok], [B.nbeta])

        if STOP <= 2:
            return
        for h in range(8):
            KT = KTp.get()
            VT = VTp.get()
            if kind == 'p':
                kbase = t0
                if t0 > 0:
                    DMA('sp', KT[:, 0:t0], kT_scr[l, h][:, 0:t0], [kscrB[l][h]], [KT], KT.name + 'ld')
                    DMA('sp', VT[:, 0:t0 // 128, :], v_scr[l, h][:, 0:t0 // 128, :],
                        [kscrB[l][h]], [VT], VT.name + 'ld')
            else:
                kbase = PAST
                DMA('sp', ktok32[:, 0:PS_, :], ck_d[l][:, h * 128:(h + 1) * 128].rearrange("(p s) d -> p s d", s=PS_),
                    [], [ktok32], 'ktok32')
                DMA('pool', VT[:, 0:PS_, :], cv_d[l][:, h * 128:(h + 1) * 128].rearrange("(p s) d -> p s d", s=PS_),
                    [], [VT], VT.name + 'ld')
                for sg in range(PS_ // 4):
                    ps = psum.get()
                    for j in range(4):
                        TR(ps[:, j * 128:(j + 1) * 128], ktok32[:, sg * 4 + j, :], ident(128), [ktok32, cst], [ps])
                    if sg % 2 == 0:
                        ACT(KT[:, sg * 512:(sg + 1) * 512], ps[:, :], AF.Copy, [ps], [KT])
                    else:
                        CP(KT[:, sg * 512:(sg + 1) * 512], ps[:, :], [ps], [KT])
            wq = wload('in', l, 0 + h)
            wk = wload('in', l, 8 + h)
            wv = wload('in', l, 16 + h)
            wg = wload('in', l, 24 + h)
            ps = proj_fm(B, wq, B.xTb)
            ACT(B.qT[:, 0:n], ps[:, 0:n], AF.Copy, [ps], [B.qT])
            ps = proj_fm(B, wk, B.xTb)
            ACT(B.kT32[:, 0:n], ps[:, 0:n], AF.Copy, [ps], [B.kT32])
            CP(KT[:, kbase:kbase + n], ps[:, 0:n], [ps], [KT])
            ps = proj_fm(B, wv, B.xTb)
            ACT(B.vT32[:, 0:n], ps[:, 0:n], AF.Copy, [ps], [B.vT32])
            ps = proj_fm(B, wg, B.xTb)
            ACT(B.sga[:, 0:n], ps[:, 0:n], AF.Silu, [ps], [B.sga])
            ks_, vs_ = kst.get(), vst.get()
            vb0 = kbase // 128
            for sub in range(nsub):
                cs = slice(sub * 128, sub * 128 + sp)
                ps = psum.get()
                TR(ps[0:sp, 0:128], B.kT32[:, cs], ident(128), [B.kT32, cst], [ps])
                TR(ps[0:sp, 128:256], B.vT32[:, cs], ident(128), [B.vT32, cst], [ps])
                ACT(ks_[0:sp, sub, :], ps[0:sp, 0:128], AF.Copy, [ps], [ks_])
                ACT(vs_[0:sp, sub, :], ps[0:sp, 128:256], AF.Copy, [ps], [vs_])
                CP(VT[0:sp, vb0 + sub, :], ps[0:sp, 128:256], [ps], [VT])
            DMA('sp', k_o[kind][l, t0:t0 + n, h * 128:(h + 1) * 128].rearrange("(s p) d -> p s d", p=sp),
                ks_[0:sp, 0:nsub, :], [ks_, OUTB], [], ks_.name + 'st')
            DMA('sp', v_o[kind][l, t0:t0 + n, h * 128:(h + 1) * 128].rearrange("(s p) d -> p s d", p=sp),
                vs_[0:sp, 0:nsub, :], [vs_, OUTB], [], vs_.name + 'st')
            if kind == 'p' and not last_tile:
                DMA('sp', kT_scr[l, h][:, t0:t0 + n], KT[:, t0:t0 + n], [KT], [kscrB[l][h]], KT.name + 'st')
                DMA('sp', v_scr[l, h][:, vb0:vb0 + nsub, :], VT[:, vb0:vb0 + nsub, :],
                    [VT], [kscrB[l][h]], VT.name + 'st')
            ps = psum.get()
            MM(ps[:, 0:n], cst[0:8, C_SEL + h * 128:C_SEL + (h + 1) * 128], B.scT[0:8, 0:n], [cst, B.scT], [ps])
            ACT(B.cqb[:, 0:n], ps[:, 0:n], AF.Copy, [ps], [B.cqb])
            if kind == 'p':
                TTo(B.cqm[:, 0:n], ps[:, 0:n], negA_rep[:, 0:n], ALU.add, [ps, negA_rep], [B.cqm])
            else:
                TTo(B.cqm[0:sp, 0:n], ps[0:sp, 0:n], negA_rep[0:sp, 0:n], ALU.add, [ps, negA_rep], [B.cqm])
            kts = []
            if kind == 'p':
                for kt in range((t0 + n) // 128):
                    i = kt - t0 // 128
                    kts.append((kt * 128, 128, kt, 0 if i < 0 else i * 128, i >= 0, B.ncall[:, l, kt, h:h + 1], B.ncall))
            else:
                for s_ in range(PS_):
                    kts.append((s_ * 128, 128, s_, 0, False, ncpast[:, h, s_:s_ + 1], ncpast))
                kts.append((PAST, sp, PS_, 0, True, B.ncall[0:sp, l, 0, h:h + 1], B.ncall))
            po, pd = ps_acc0, ps_acc1
            SKEW = 2
            pend = []
            nkts = len(kts)

            def pv_stage(item):
                ii_, np__, vti_, qlo_, pT_ = item
                MM(po[:, qlo_:n], VT[0:np__, vti_, :], pT_[0:np__, qlo_:n], [VT, pT_], [po],
                   start=(ii_ == 0), stop=(ii_ == nkts - 1))
                MM(pd[:, qlo_:n], ones_bf[0:np__, :], pT_[0:np__, qlo_:n], [ones_bf, pT_], [pd],
                   start=(ii_ == 0), stop=(ii_ == nkts - 1))

            for ii, (kc0, np_, vti, qlo, diag, nck, nckT) in enumerate(kts):
                pss = psum.get()
                MM(pss[0:np_, qlo:n], KT[:, kc0:kc0 + np_], B.qT[:, qlo:n], [KT, B.qT], [pss])
                t = tpool.get()
                if diag:
                    w_ = min(128, n - qlo)
                    STT(t[0:np_, qlo:qlo + w_], pss[0:np_, qlo:qlo + w_], FOX_SCALE, B.cqm[0:np_, qlo:qlo + w_],
                        ALU.mult, ALU.add, [pss, B.cqm], [t])
                    if qlo + w_ < n:
                        STT(t[0:np_, qlo + w_:n], pss[0:np_, qlo + w_:n], FOX_SCALE, B.cqb[0:np_, qlo + w_:n],
                            ALU.mult, ALU.add, [pss, B.cqb], [t])
                else:
                    STT(t[0:np_, 0:n], pss[0:np_, 0:n], FOX_SCALE, B.cqb[0:np_, 0:n], ALU.mult, ALU.add,
                        [pss, B.cqb], [t])
                pT = ppool.get()
                ACT(pT[0:np_, qlo:n], t[0:np_, qlo:n], AF.Exp, [t, nckT], [pT], bias=nck)
                pend.append((ii, np_, vti, qlo, pT))
                if len(pend) > SKEW:
                    pv_stage(pend.pop(0))
            while pend:
                pv_stage(pend.pop(0))
            rden = f2k.get()
            RECIP(rden[:, 0:n], pd[:, 0:n], [pd], [rden])
            t2 = f2k.get()
            TTo(t2[:, 0:n], po[:, 0:n], rden[:, 0:n], ALU.mult, [po, rden], [t2])
            TTo(B.gaT[:, h, 0:n], t2[:, 0:n], B.sga[:, 0:n], ALU.mult, [t2, B.sga], [B.gaT.bs[h]])

        if STOP <= 3:
            return
        emit_conv_out = (kind == 's') or last_tile
        for h in range(8):
            wts = [wload('in', l, 32 + j * 8 + h) for j in range(3)]
            wgb = wload('in', l, 56 + h)
            post = []
            for j in range(3):
                cid = j * 8 + h
                ps = proj_fm(B, wts[j], B.xTb)
                xp = xppool.get()
                CP(xp[:, 0:3], B.convc[:, l, cid, :], [B.convc], [xp])
                ACT(xp[:, 3:3 + n], ps[:, 0:n], AF.Copy, [ps], [xp])
                CP(B.convc[:, l, cid, :], xp[:, n:n + 3], [xp], [B.convc])
                if emit_conv_out:
                    ps2 = psum.get()
                    TR(ps2[0:3, 0:128], xp[:, n:n + 3], ident(128), [xp, cst], [ps2])
                    CP(cvst[0:3, cid * 128:(cid + 1) * 128], ps2[0:3, 0:128], [ps2], [cvst])
                acc = f2k.get()
                cw = lambda i: smp[:, sb0 + 4 + cid * 4 + i:sb0 + 4 + cid * 4 + i + 1]
                TS(acc[:, 0:n], xp[:, 0:n], cw(0), ALU.mult, [xp, smp], [acc])
                for i in range(1, 4):
                    STT(acc[:, 0:n], xp[:, i:i + n], cw(i), acc[:, 0:n], ALU.mult, ALU.add, [xp, smp, acc], [acc])
                ACT(acc[:, 0:n], acc[:, 0:n], AF.Silu, [acc], [acc])
                post.append(acc)
            aq, ak, av = post
            kT32g = f2k.get()
            for (a_, scl, outs) in ((aq, GDN_SCALE, 'q'), (ak, 1.0, 'k')):
                sq = b1k.get()
                ACT(sq[:, 0:n], a_[:, 0:n], AF.Square, [a_], [sq])
                ps = psum.get()
                MM(ps[:, 0:n], ones_bf[:, :], sq[:, 0:n], [ones_bf, sq], [ps])
                r = B.stat[0]
                ACT(r[:, 0:n], ps[:, 0:n], AF.Sqrt, [ps], [r], bias=RMS_EPS)
                RECIP(r[:, 0:n], r[:, 0:n], [r], [r])
                if outs == 'q':
                    STT(B.qTn[:, 0:n], a_[:, 0:n], scl, r[:, 0:n], ALU.mult, ALU.mult, [a_, r], [B.qTn])
                else:
                    TTo(kT32g[:, 0:n], a_[:, 0:n], r[:, 0:n], ALU.mult, [a_, r], [kT32g])
                    ACT(B.kTn[:, 0:n], kT32g[:, 0:n], AF.Copy, [kT32g], [B.kTn])
            ps = proj_fm(B, wgb, B.xTb)
            sgb = f2k.get()
            ACT(sgb[:, 0:n], ps[:, 0:n], AF.Silu, [ps], [sgb])
            for sub in range(nsub):
                cs = slice(sub * 128, sub * 128 + sp)
                ps = psum.get()
                TR(ps[0:sp, 0:128], kT32g[:, cs], ident(128), [kT32g, cst], [ps])
                TR(ps[0:sp, 128:256], av[:, cs], ident(128), [av, cst], [ps])
                TS(B.kbg[0:sp, sub, :], ps[0:sp, 0:128], B.bege[0:sp, sub, h:h + 1], ALU.mult, [ps, B.bege], [B.kbg])
                TS(B.ktl[0:sp, sub, :], ps[0:sp, 0:128], B.tail[0:sp, sub, h:h + 1], ALU.mult, [ps, B.tail], [B.ktl])
                TS(B.vb[0:sp, sub, :], ps[0:sp, 128:256], B.smtok[0:sp, sub, 64 + h:65 + h], ALU.mult,
                   [ps, B.smtok], [B.vb])
            Sbuf = B.S.bs[ls * 8 + h]
            ACT(Sb[:, h, :], B.S[:, ls, h, :], AF.Copy, [Sbuf], [Sb.bs[h]])
            po = ps_acc0
            for sub in range(nsub):
                cs = slice(sub * 128, sub * 128 + sp)
                psb = psum.get()
                MM(psb[:, 0:sp], cst[0:8, C_SEL + h * 128:C_SEL + (h + 1) * 128], B.gcT[0:8, cs], [cst, B.gcT], [psb])
                MM(psb[0:sp, 128:128 + sp], B.kTn[:, cs], B.kTn[:, cs], [B.kTn], [psb])
                MM(psb[0:sp, 256:256 + sp], B.kTn[:, cs], B.qTn[:, cs], [B.kTn, B.qTn], [psb])
                tL = m32.get()
                TTo(tL[0:sp, 0:sp], psb[0:sp, 0:sp], cst[0:sp, C_POSLS:C_POSLS + sp], ALU.add, [psb, cst], [tL])
                ACT(tL[0:sp, 0:sp], tL[0:sp, 0:sp], AF.Exp, [tL, B.gctok], [tL], bias=B.gctok[0:sp, sub, h:h + 1], scale=-1.0)
                tU = m32.get()
                TTo(tU[0:sp, 0:sp], psb[0:sp, 0:sp], cst[0:sp, C_NEGU:C_NEGU + sp], ALU.add, [psb, cst], [tU])
                ACT(tU[0:sp, 0:sp], tU[0:sp, 0:sp], AF.Exp, [tU, B.ngctok], [tU], bias=B.ngctok[0:sp, sub, h:h + 1])
                egb = m32.get()
                ACT(egb[:, 0:sp], psb[:, 0:sp], AF.Exp, [psb], [egb])
                V32 = m32.get()
                STT(V32[0:sp, 0:sp], psb[0:sp, 128:128 + sp], B.nbeta[0:sp, sub, h:h + 1], tL[0:sp, 0:sp],
                    ALU.mult, ALU.mult, [psb, B.nbeta, tL], [V32])
                Vb = mbf.get()
                ACT(Vb[0:sp, 0:sp], V32[0:sp, 0:sp], AF.Copy, [V32], [Vb])
                QKTb = gbf.get()
                TTo(QKTb[0:sp, 0:sp], psb[0:sp, 256:256 + sp], tU[0:sp, 0:sp], ALU.mult, [psb, tU], [QKTb])
                qdT = gbf.get()
                TTo(qdT[:, 0:sp], B.qTn[:, cs], egb[:, 0:sp], ALU.mult, [B.qTn, egb], [qdT])
                psu = psum.get()
                TR(psu[0:sp, 0:sp], V32[0:sp, 0:sp], ident(sp), [V32, cst], [psu])
                Ub = mbf.get()
                ACT(Ub[0:sp, 0:sp], psu[0:sp, 0:sp], AF.Copy, [psu], [Ub])
                R32 = m32.get()
                TTo(R32[0:sp, 0:sp], psu[0:sp, 0:sp], ident(sp), ALU.add, [psu, cst], [R32])
                Rb = mbf.get()
                ACT(Rb[0:sp, 0:sp], R32[0:sp, 0:sp], AF.Copy, [R32], [Rb])
                pw = 1
                while 2 * pw < C:
                    lastit = 4 * pw >= C
                    ps2 = psum.get()
                    MM(ps2[0:sp, 0:sp], Ub[0:sp, 0:sp], Vb[0:sp, 0:sp], [Ub, Vb], [ps2])
                    if not lastit:
                        MM(ps2[0:sp, 128:128 + sp], Vb[0:sp, 0:sp], Ub[0:sp, 0:sp], [Ub, Vb], [ps2])
                    Vb2 = mbf.get()
                    ACT(Vb2[0:sp, 0:sp], ps2[0:sp, 0:sp], AF.Copy, [ps2], [Vb2])
                    if not lastit:
                        Ub2 = mbf.get()
                        CP(Ub2[0:sp, 0:sp], ps2[0:sp, 128:128 + sp], [ps2], [Ub2])
                        Ub = Ub2
                    Vb = Vb2
                    ps3 = psum.get()
                    MM(ps3[0:sp, 0:sp], Vb[0:sp, 0:sp], Rb[0:sp, 0:sp], [Vb, Rb], [ps3])
                    TTo(R32[0:sp, 0:sp], R32[0:sp, 0:sp], ps3[0:sp, 0:sp], ALU.add, [R32, ps3], [R32])
                    Rb = mbf.get()
                    ACT(Rb[0:sp, 0:sp], R32[0:sp, 0:sp], AF.Copy, [R32], [Rb])
                    pw *= 2
                psw = psum.get()
                MM(psw[0:sp, 0:128], Rb[0:sp, 0:sp], B.vb[0:sp, sub, :], [Rb, B.vb], [psw])
                MM(psw[:, 128:128 + sp], B.kbg[0:sp, sub, :], Rb[0:sp, 0:sp], [Rb, B.kbg], [psw])
                u_sb = m32.get()
                ACT(u_sb[0:sp, :], psw[0:sp, 0:128], AF.Copy, [psw], [u_sb])
                wTb = gbf.get()
                CP(wTb[:, 0:sp], psw[:, 128:128 + sp], [psw], [wTb])
                for ci in range(sp // C):
                    co = ci * C
                    psv = psum.get()
                    MM(psv[0:sp, 0:128], wTb[:, 0:sp], Sb[:, h, :], [wTb, Sb.bs[h]], [psv])
                    vnew = gbf.get()
                    TTo(vnew[co:co + C, :], u_sb[co:co + C, :], psv[co:co + C, 0:128], ALU.subtract, [u_sb, psv], [vnew])
                    oc = slice(sub * 128 + co, sub * 128 + co + C)
                    MM(po[:, oc], Sb[:, h, :], qdT[:, co:co + C], [Sb.bs[h], qdT], [po], start=True, stop=False)
                    MM(po[:, oc], vnew[co:co + C, :], QKTb[co:co + C, co:co + C], [vnew, QKTb], [po], start=False, stop=True)
                    pss = psum.get()
                    MM(pss[:, 0:128], B.ktl[co:co + C, sub, :], vnew[co:co + C, :], [B.ktl, vnew], [pss])
                    STT(B.S[:, ls, h, :], B.S[:, ls, h, :], B.glb[:, sub, ci, h:h + 1], pss[:, 0:128], ALU.mult, ALU.add,
                        [Sbuf, B.glb, pss], [Sbuf])
                    ACT(Sb[:, h, :], B.S[:, ls, h, :], AF.Copy, [Sbuf], [Sb.bs[h]])
            if DBG is not None:
                CP(DBG[kind][:, h, 0:n], po[:, 0:n], [po], [DBG[kind]])
            sq = b1k.get()
            ACT(sq[:, 0:n], po[:, 0:n], AF.Square, [po], [sq])
            ps = psum.get()
            MM(ps[:, 0:n], ones_bf[:, :], sq[:, 0:n], [ones_bf, sq], [ps])
            r = B.stat[0]
            ACT(r[:, 0:n], ps[:, 0:n], AF.Sqrt, [ps], [r], bias=RMS_EPS, scale=1.0 / 128)
            RECIP(r[:, 0:n], r[:, 0:n], [r], [r])
            on = f2k.get()
            TTo(on[:, 0:n], po[:, 0:n], r[:, 0:n], ALU.mult, [po, r], [on])
            STT(B.gbT[:, h, 0:n], on[:, 0:n], smp[:, sb0 + 3:sb0 + 4], sgb[:, 0:n], ALU.mult, ALU.mult,
                [on, smp, sgb], [B.gbT.bs[h]])
        if emit_conv_out:
            DMA('sp', cv_o[kind][l], cvst[0:3, :], [cvst, OUTB], [], 'cvst_st')
            DMA('sp', st_o[kind][l].rearrange("h k v -> k h v"), B.S[:, ls, :, :], B.S.bs[ls * 8:(ls + 1) * 8] + [OUTB], [],
                kind + 'S_st')

        if STOP <= 4:
            return
        for j in range(16):
            cj = slice(j * 128, (j + 1) * 128)
            wma = wload('in', l, 64 + j)
            wmb = wload('in', l, 80 + j)
            ps_ma = proj_fm(B, wma, B.xTb)
            sa = f2k.get()
            ACT(sa[:, 0:n], ps_ma[:, 0:n], AF.Sigmoid, [ps_ma], [sa])
            ps_mb = proj_fm(B, wmb, B.xTb)
            sb_ = f2k.get()
            ACT(sb_[:, 0:n], ps_mb[:, 0:n], AF.Sigmoid, [ps_mb], [sb_])
            wfa = wload('of', l, j, kc=8)
            wfb = wload('og', l, j, kc=8)
            ps_ya = proj_fm(B, wfa, B.gaT, kc=8)
            TTo(sa[:, 0:n], sa[:, 0:n], ps_ya[:, 0:n], ALU.mult, [sa, ps_ya], [sa])
            ps_yb = proj_fm(B, wfb, B.gbT, kc=8)
            TTo(sb_[:, 0:n], sb_[:, 0:n], ps_yb[:, 0:n], ALU.mult, [sb_, ps_yb], [sb_])
            TTo(B.mT[:, j, 0:n], sa[:, 0:n], sb_[:, 0:n], ALU.add, [sa, sb_], [B.mT.bs[j]])
        if STOP <= 5:
            return
        for j in range(16):
            wj = wload('o', l, j)
            ps = proj_fm(B, wj, B.mT)
            STT(B.xT32[:, j, 0:n], B.xT32[:, j, 0:n], ALPHA, ps[:, 0:n], ALU.mult, ALU.add,
                [B.xT32.bs[j], ps], [B.xT32.bs[j]])
        pm, pq = ps_acc0, ps_acc1
        for j in range(16):
            ACT(B.xTb[:, j, 0:n], B.xT32[:, j, 0:n], AF.Copy, [B.xT32.bs[j]], [B.xTb.bs[j]])
            ACT(B.mT[:, j, 0:n], B.xT32[:, j, 0:n], AF.Square, [B.xT32.bs[j]], [B.mT.bs[j]])
            MM(pm[:, 0:n], ones_bf[:, :], B.xTb[:, j, 0:n], [ones_bf, B.xTb.bs[j]], [pm], start=(j == 0), stop=(j == 15))
            MM(pq[:, 0:n], ones_bf[:, :], B.mT[:, j, 0:n], [ones_bf, B.mT.bs[j]], [pq], start=(j == 0), stop=(j == 15))
        mean, rstd, tmpv = B.stat
        ACT(mean[:, 0:n], pm[:, 0:n], AF.Copy, [pm], [mean], scale=1.0 / D)
        TTo(tmpv[:, 0:n], mean[:, 0:n], mean[:, 0:n], ALU.mult, [mean], [tmpv])
        STT(tmpv[:, 0:n], pq[:, 0:n], 1.0 / D, tmpv[:, 0:n], ALU.mult, ALU.subtract, [pq, tmpv], [tmpv])
        ACT(rstd[:, 0:n], tmpv[:, 0:n], AF.Sqrt, [tmpv], [rstd], bias=LN_EPS)
        RECIP(rstd[:, 0:n], rstd[:, 0:n], [rstd], [rstd])
        for j in range(16):
            t_ = f2k.get()
            TTo(t_[:, 0:n], B.xT32[:, j, 0:n], mean[:, 0:n], ALU.subtract, [B.xT32.bs[j], mean], [t_])
            TTo(t_[:, 0:n], t_[:, 0:n], rstd[:, 0:n], ALU.mult, [t_, rstd], [t_])
            ACT(B.xT32[:, j, 0:n], t_[:, 0:n], AF.Identity, [t_, smp], [B.xT32.bs[j]],
                bias=smp[:, sb0 + 116 + j:sb0 + 117 + j], scale=smp[:, sb0 + 100 + j:sb0 + 101 + j])
            CP(B.xTb[:, j, 0:n], B.xT32[:, j, 0:n], [B.xT32.bs[j]], [B.xTb.bs[j]])
        if STOP <= 6:
            return
        DMA('sp', B.ptok[0:sp, :, :], p_d[kind][l, t0:t0 + n, :].rearrange("(s p) c -> p s c", p=sp), [], [B.ptok],
            kind + 'ptok')
        for sub in range(nsub):
            ps = psum.get()
            TR(ps[:, 0:sp], B.ptok[0:sp, sub, 0:128], ident(sp), [B.ptok, cst], [ps])
            TR(ps[:, 128:128 + sp], B.ptok[0:sp, sub, 128:256], ident(sp), [B.ptok, cst], [ps])
            CP(B.pTb[:, 0, sub * 128:sub * 128 + sp], ps[:, 0:sp], [ps], [B.pTb])
            CP(B.pTb[:, 1, sub * 128:sub * 128 + sp], ps[:, 128:128 + sp], [ps], [B.pTb])
        B.pTb.bs = [B.pTb.b, B.pTb.b]
        pe2 = ps_acc0
        for j in range(16):
            wpj = wload('pp', l, j, kc=2)
            ps = proj_fm(B, wpj, B.pTb, kc=2)
            CP(B.mT[:, j, 0:n], ps[:, 0:n], [ps], [B.mT.bs[j]])
            sq = b1k.get()
            ACT(sq[:, 0:n], ps[:, 0:n], AF.Square, [ps], [sq])
            MM(pe2[:, 0:n], ones_bf[:, :], sq[:, 0:n], [ones_bf, sq], [pe2], start=(j == 0), stop=(j == 15))
        rse = B.stat[0]
        ACT(rse[:, 0:n], pe2[:, 0:n], AF.Sqrt, [pe2], [rse], bias=RMS_EPS, scale=1.0 / D)
        RECIP(rse[:, 0:n], rse[:, 0:n], [rse], [rse])
        for j in range(16):
            wg = wload('pg', l, j)
            ps = proj_fm(B, wg, B.xTb)
            sg_ = f2k.get()
            ACT(sg_[:, 0:n], ps[:, 0:n], AF.Sigmoid, [ps], [sg_])
            t_ = f2k.get()
            TTo(t_[:, 0:n], B.mT[:, j, 0:n], rse[:, 0:n], ALU.mult, [B.mT.bs[j], rse], [t_])
            STT(t_[:, 0:n], t_[:, 0:n], smp[:, sb0 + 132 + j:sb0 + 133 + j], sg_[:, 0:n], ALU.mult, ALU.mult,
                [t_, smp, sg_], [t_])
            TTo(B.xT32[:, j, 0:n], B.xT32[:, j, 0:n], t_[:, 0:n], ALU.add, [B.xT32.bs[j], t_], [B.xT32.bs[j]])
        for j in range(16):
            ACT(B.xTb[:, j, 0:n], B.xT32[:, j, 0:n], AF.Copy, [B.xT32.bs[j]], [B.xTb.bs[j]])
        prepass(l + 1)

    if os.environ.get('MK_PONLY') is None:
      load_tile('s', 0)
      if STOP > 1:
        for l in range(DEPTH):
            layer('s', l, 0, True)
      store_tile('s', 0)
    for Ti in range(NTILES if os.environ.get('MK_SONLY') is None else 0):
        if os.environ.get('MK_NOLOAD') is None:
            load_tile('p', Ti * NT)
        if STOP > 1:
            for l in range(DEPTH):
                layer('p', l, Ti * NT, Ti == NTILES - 1)
        if os.environ.get('MK_NOSTORE') is None:
            store_tile('p', Ti * NT)
    P.op('sp', None, [], [OUTB])
    P.emit(nc, es)
    es.close()
    return nc, P


_CACHE = {}


def _get_prog(DEPTH, SEQ, PAST, NT, DS):
    key = (DEPTH, SEQ, PAST, NT, DS)
    if key not in _CACHE:
        _CACHE[key] = build(DEPTH, SEQ, PAST, NT, DS)
    return _CACHE[key]


def kernel(x_prompt, x_sample, cache_fox_k, cache_fox_v, cache_fox_logf, state_gdn_conv, state_gdn,
           p_prompt, p_sample, w_in, fox_f_bias, gdn_conv_w, gdn_a_log, gdn_dt_bias, gdn_norm_w,
           w_out_fox, w_out_gdn, w_out, ln_g, ln_b, w_pl_proj, w_pl_gate, pl_norm_w, NT=None, n_cores=None):
    f = lambda a: np.ascontiguousarray(np.asarray(a, dtype=np.float32))
    x_prompt, x_sample = f(x_prompt), f(x_sample)
    BATCH, SEQ, _ = x_prompt.shape
    DS = x_sample.shape[1]
    DEPTH = w_in.shape[0]
    PAST = cache_fox_k.shape[2]
    if NT is None:
        NT = 256
    nco = BATCH if n_cores is None else n_cores
    nc, _P = _get_prog(DEPTH, SEQ, PAST, NT, DS)
    inp = dict(fox_f_bias=f(fox_f_bias), gdn_dt_bias=f(gdn_dt_bias), gdn_a_log=f(gdn_a_log), gdn_norm_w=f(gdn_norm_w),
               gdn_conv_w=f(gdn_conv_w), ln_g=f(ln_g), ln_b=f(ln_b), pl_norm_w=f(pl_norm_w))
    def tile_w(w, kc):
        L_, K_, N_ = w.shape
        return np.ascontiguousarray(w.reshape(L_, kc, 128, N_ // 128, 128).transpose(0, 3, 2, 1, 4)).reshape(
            L_, N_ // 128, 128, kc * 128)
    w_in = f(w_in)
    w_main = np.concatenate([w_in[:, :, 0:OFF_F], w_in[:, :, OFF_GA:OFF_A], w_in[:, :, OFF_GB:NIN]], axis=2)
    wsm_h = np.zeros((DEPTH, D, 96), np.float32)
    wsm_h[:, :, 0:8] = w_in[:, :, OFF_F:OFF_F + 8]
    wsm_h[:, :, 32:40] = w_in[:, :, OFF_A:OFF_A + 8]
    wsm_h[:, :, 64:72] = w_in[:, :, OFF_BETA:OFF_BETA + 8]
    wsm_h = np.ascontiguousarray(wsm_h.reshape(DEPTH, 16, 128, 96).transpose(0, 2, 1, 3)).reshape(DEPTH, 128, 16 * 96)
    shared = dict(wt_in=tile_w(w_main, 16), wt_of=tile_w(f(w_out_fox), 8), wt_og=tile_w(f(w_out_gdn), 8),
                  wt_o=tile_w(f(w_out), 16), wt_pp=tile_w(f(w_pl_proj), 2), wt_pg=tile_w(f(w_pl_gate), 16),
                  wsm_d=wsm_h, smallp=make_smallp(inp, DEPTH), consts=make_consts())
    del w_main
    ck, cv, clf = f(cache_fox_k), f(cache_fox_v), f(cache_fox_logf)
    scv, sg, pp, ps = f(state_gdn_conv), f(state_gdn), f(p_prompt), f(p_sample)
    in_maps = []
    for b in range(nco):
        m = dict(shared)
        m.update(xp=x_prompt[b], xs=x_sample[b],
                 ck=np.ascontiguousarray(ck[:, b].reshape(DEPTH, PAST, 1024)),
                 cv=np.ascontiguousarray(cv[:, b].reshape(DEPTH, PAST, 1024)),
                 clf=np.ascontiguousarray(clf[:, b]), scv=np.ascontiguousarray(scv[:, b]),
                 sg=np.ascontiguousarray(sg[:, b]), pp=np.ascontiguousarray(pp[:, b]),
                 ps=np.ascontiguousarray(ps[:, b]))
        in_maps.append(m)
    res = run_bass_kernel_spmd(nc, in_maps, core_ids=list(range(nco)))
    R = res.results
    st = lambda name, ax: np.stack([np.asarray(R[b][name]) for b in range(nco)], axis=ax)
    yp = st('yp', 0)
    ys = st('ys', 0)
    kp = st('kp', 1).reshape(DEPTH, nco, SEQ, 8, 128)
    vp = st('vp', 1).reshape(DEPTH, nco, SEQ, 8, 128)
    lfp = st('lfp', 1)
    cvp = st('cvp', 1)
    stp = st('stp', 1)
    ks = st('ks', 1).reshape(DEPTH, nco, DS, 8, 128)
    vs = st('vs', 1).reshape(DEPTH, nco, DS, 8, 128)
    lfs = st('lfs', 1)
    cvs = st('cvs', 1)
    sts = st('sts', 1)
    return (yp, ys, kp, vp, lfp, cvp, stp, ks, vs, lfs, cvs, sts)
```
